# Optimizing a Trainium2 kernel written in Bass

```python
import math
import jax
import jax.numpy as jnp
from jax import lax
import numpy as np

D_MODEL = 1024
BATCH = 4
SEQ = 8192
DEPTH = 4

N_EVEN = (DEPTH + 1) // 2
N_ODD = DEPTH // 2
RMS_EPS = 1e-6

GDN_HEADS = 4
GDN_DK = 128
GDN_DV = 128
GDN_QK_W = GDN_HEADS * GDN_DK
GDN_V_W = GDN_HEADS * GDN_DV
GDN_CONV = 4
GDN_CHUNK = 64

POOL_WINDOWS = (2, 4, 8, 16)
POOL_GROUPS = len(POOL_WINDOWS)
POOL_GROUP_W = 128
POOL_W = POOL_GROUPS * POOL_GROUP_W

EVEN_QKV = 2 * GDN_QK_W + GDN_V_W
EVEN_IN = EVEN_QKV + GDN_V_W + 2 * GDN_HEADS + POOL_W
EVEN_OUT = GDN_V_W + POOL_W

DIL_PATTERNS = ((128, 1), (512, 4), (2048, 16))
ATT_GROUPS = len(DIL_PATTERNS)
ATT_HEADS = 8
ATT_DH = 128
ATT_W = ATT_HEADS * ATT_DH
ODD_IN = ATT_GROUPS * 3 * ATT_W
ATT_BLOCK = 128

D_FF = 2816
FFN_CONV = 3

kernel_name = "hybrid_gdn_pool_dilated_alibi_convffn"


def rms_norm(x, gain):
    xf = x.astype(jnp.float32)
    y = xf * lax.rsqrt(jnp.mean(xf * xf, axis=-1, keepdims=True) + RMS_EPS)
    return (y * gain.astype(jnp.float32)).astype(x.dtype)


def l2_norm(x):
    xf = x.astype(jnp.float32)
    return xf * lax.rsqrt(jnp.sum(xf * xf, axis=-1, keepdims=True) + RMS_EPS)


def causal_dwconv(x, w):
    K, C = w.shape
    return lax.conv_general_dilated(
        x, w[:, None, :].astype(x.dtype), window_strides=(1,), padding=[(K - 1, 0)],
        dimension_numbers=("NWC", "WIO", "NWC"), feature_group_count=C)


def gated_delta_rule(q, k, v, g, beta):
    B, T, H, dk = q.shape
    dv = v.shape[-1]
    C = GDN_CHUNK
    N = T // C
    f32 = jnp.float32

    def chunks(t):
        return t.astype(f32).reshape(B, N, C, H, -1).transpose(0, 3, 1, 2, 4)

    q = chunks(q) * (dk ** -0.5)
    k = chunks(k)
    v = chunks(v)
    g = g.astype(f32).reshape(B, N, C, H).transpose(0, 3, 1, 2)
    beta = beta.astype(f32).reshape(B, N, C, H).transpose(0, 3, 1, 2)
    gc = jnp.cumsum(g, axis=-1)

    idx = jnp.arange(C)
    causal = idx[:, None] >= idx[None, :]
    strict = idx[:, None] > idx[None, :]
    diff = gc[..., :, None] - gc[..., None, :]
    decay = jnp.where(causal, jnp.exp(jnp.where(causal, diff, 0.0)), 0.0)

    kb = k * beta[..., None]
    L = jnp.where(strict, jnp.einsum("bhncd,bhnsd->bhncs", kb, k) * decay, 0.0)
    rhs = jnp.concatenate([v * beta[..., None], kb * jnp.exp(gc)[..., None]], axis=-1)
    uw = lax.linalg.triangular_solve(L, rhs, left_side=True, lower=True,
                                     unit_diagonal=True)
    u, w = uw[..., :dv], uw[..., dv:]

    intra = jnp.where(causal, jnp.einsum("bhncd,bhnsd->bhncs", q, k) * decay, 0.0)
    qg = q * jnp.exp(gc)[..., None]
    kdec = k * jnp.exp(gc[..., -1:] - gc)[..., None]
    glast = jnp.exp(gc[..., -1])

    def step(S, xs):
        qg_i, kdec_i, u_i, w_i, intra_i, gl_i = xs
        v_new = u_i - jnp.einsum("bhck,bhkv->bhcv", w_i, S)
        o_i = (jnp.einsum("bhck,bhkv->bhcv", qg_i, S)
               + jnp.einsum("bhcs,bhsv->bhcv", intra_i, v_new))
        S = S * gl_i[..., None, None] + jnp.einsum("bhck,bhcv->bhkv", kdec_i, v_new)
        return S, o_i

    xs = tuple(jnp.moveaxis(t, 2, 0) for t in (qg, kdec, u, w, intra, glast))
    S0 = jnp.zeros((B, H, dk, dv), f32)
    _, o = lax.scan(step, S0, xs)
    return o.transpose(1, 0, 3, 2, 4).reshape(B, T, H, dv)


def multiscale_pool(p, pool_w, pool_scale):
    B, T, _ = p.shape
    pf = p.astype(jnp.float32).reshape(B, T, POOL_GROUPS, POOL_GROUP_W)
    csum = jnp.cumsum(pf, axis=1)
    t1 = jnp.arange(1, T + 1, dtype=jnp.float32)
    pooled = []
    for gi, win in enumerate(POOL_WINDOWS):
        cg = csum[:, :, gi]
        lag = jnp.pad(cg, ((0, 0), (win, 0), (0, 0)))[:, :T]
        cnt = jnp.minimum(t1, float(win))[None, :, None]
        pooled.append((cg - lag) / cnt)
    pooled = jnp.stack(pooled, axis=2) - pf
    y = jnp.einsum("btgc,gcd->btgd", pooled, pool_w.astype(jnp.float32))
    return (y.reshape(B, T, POOL_W) * pool_scale.astype(jnp.float32)).astype(p.dtype)


def even_mixer(h, w_in, w_out, conv_w, a_log, dt_bias, gdn_norm, pool_w, pool_scale):
    B, T, _ = h.shape
    proj = h @ w_in
    i1 = EVEN_QKV
    i2 = i1 + GDN_V_W
    i3 = i2 + GDN_HEADS
    i4 = i3 + GDN_HEADS
    qkv, z, b_raw, a_raw, pool_in = jnp.split(proj, [i1, i2, i3, i4], axis=-1)

    qkv = jax.nn.silu(causal_dwconv(qkv, conv_w))
    q, k, v = jnp.split(qkv, [GDN_QK_W, 2 * GDN_QK_W], axis=-1)
    q = l2_norm(q.reshape(B, T, GDN_HEADS, GDN_DK))
    k = l2_norm(k.reshape(B, T, GDN_HEADS, GDN_DK))
    v = v.reshape(B, T, GDN_HEADS, GDN_DV)
    beta = jax.nn.sigmoid(b_raw.astype(jnp.float32))
    g = -jnp.exp(a_log.astype(jnp.float32)) * jax.nn.softplus(
        a_raw.astype(jnp.float32) + dt_bias.astype(jnp.float32))
    o = gated_delta_rule(q, k, v, g, beta)
    o = rms_norm(o, gdn_norm) * jax.nn.silu(
        z.astype(jnp.float32).reshape(B, T, GDN_HEADS, GDN_DV))
    o_a = o.reshape(B, T, GDN_V_W).astype(h.dtype)

    o_b = multiscale_pool(pool_in, pool_w, pool_scale)
    return jnp.concatenate([o_a, o_b], axis=-1) @ w_out


def alibi_slopes(n_heads):
    return jnp.exp2(-8.0 * jnp.arange(1, n_heads + 1, dtype=jnp.float32) / n_heads)


def dilated_band_attention(q, k, v, dil, n_back, slopes):
    B, T, H, dh = q.shape
    L = T // dil
    nb = -(-L // ATT_BLOCK)
    Lp = nb * ATT_BLOCK

    def to_sub(t):
        t = t.reshape(B, L, dil, H, dh).transpose(0, 2, 3, 1, 4)
        t = jnp.pad(t, ((0, 0), (0, 0), (0, 0), (0, Lp - L), (0, 0)))
        return t.reshape(B, dil, H, nb, ATT_BLOCK, dh)

    def band(t):
        prev = jnp.pad(t, ((0, 0), (0, 0), (0, 0), (1, 0), (0, 0), (0, 0)))[:, :, :, :nb]
        return jnp.concatenate([prev, t], axis=4)

    qb = to_sub(q)
    kb = band(to_sub(k))
    vb = band(to_sub(v))

    a = jnp.arange(ATT_BLOCK)[:, None]
    j = jnp.arange(2 * ATT_BLOCK)[None, :]
    rel = ATT_BLOCK + a - j
    blk = jnp.arange(nb)[:, None, None]
    mask = (rel >= 0) & (rel <= n_back) & ((j >= ATT_BLOCK) | (blk > 0))
    bias = -(slopes * dil)[:, None, None, None] * rel.astype(jnp.float32)

    s = jnp.einsum("brhnqd,brhnkd->brhnqk", qb, kb) + bias
    s = jnp.where(mask, s, -jnp.inf)
    m = jnp.max(s, axis=-1, keepdims=True)
    p = jnp.exp(s - m)
    l = jnp.sum(p, axis=-1, keepdims=True)
    o = jnp.einsum("brhnqk,brhnkd->brhnqd", p, vb) / l
    lse = m + jnp.log(l)

    def from_sub(t):
        t = t.reshape(B, dil, H, Lp, -1)[:, :, :, :L]
        return t.transpose(0, 3, 1, 2, 4).reshape(B, T, H, -1)

    return from_sub(o), from_sub(lse)[..., 0]


def odd_mixer(h, w_in, w_out, q_norm, k_norm):
    B, T, _ = h.shape
    slopes = alibi_slopes(ATT_HEADS)
    outs, lses = [], []
    for gi, (window, dil) in enumerate(DIL_PATTERNS):
        cols = w_in[:, gi * 3 * ATT_W:(gi + 1) * 3 * ATT_W]
        proj = (h @ cols).astype(jnp.float32).reshape(B, T, 3, ATT_HEADS, ATT_DH)
        q = rms_norm(proj[:, :, 0], q_norm) * (ATT_DH ** -0.5)
        k = rms_norm(proj[:, :, 1], k_norm)
        v = proj[:, :, 2]
        o, lse = dilated_band_attention(q, k, v, dil, window // dil, slopes)
        outs.append(o)
        lses.append(lse)
    wts = jax.nn.softmax(jnp.stack(lses), axis=0)
    o = jnp.sum(wts[..., None] * jnp.stack(outs), axis=0)
    return o.reshape(B, T, ATT_W).astype(h.dtype) @ w_out


def conv_ffn(h, w_up, conv_w, conv_b, w_down):
    up = h @ w_up
    gate, val = jnp.split(up, 2, axis=-1)
    gate = causal_dwconv(gate, conv_w) + conv_b
    return (jax.nn.silu(gate) * val) @ w_down


def setup_inputs(seed: int = 0) -> dict:
    key = jax.random.key(seed)
    ks = jax.random.split(key, 24)
    f32 = jnp.float32
    D = D_MODEL

    def nrm(k, shape, s):
        return jax.random.normal(k, shape, f32) * s

    dt = jnp.exp(jax.random.uniform(ks[10], (N_EVEN, GDN_HEADS), f32,
                                    math.log(1e-3), math.log(1e-1)))
    return {
        "x": nrm(ks[0], (BATCH, SEQ, D), 1.0),
        "c": nrm(ks[1], (BATCH, D), 1.0),
        "ada_w": nrm(ks[2], (DEPTH, D, 6 * D), 0.5 * D ** -0.5),
        "ada_b": nrm(ks[3], (DEPTH, 6 * D), 0.01),
        "norm_mix": 1.0 + nrm(ks[4], (DEPTH, D), 0.02),
        "norm_ffn": 1.0 + nrm(ks[5], (DEPTH, D), 0.02),
        "ev_w_in": nrm(ks[6], (N_EVEN, D, EVEN_IN), D ** -0.5),
        "ev_w_out": nrm(ks[7], (N_EVEN, EVEN_OUT, D), EVEN_OUT ** -0.5),
        "gdn_conv_w": nrm(ks[8], (N_EVEN, GDN_CONV, EVEN_QKV), GDN_CONV ** -0.5),
        "gdn_a_log": jnp.log(jax.random.uniform(ks[9], (N_EVEN, GDN_HEADS), f32, 1.0, 16.0)),
        "gdn_dt_bias": dt + jnp.log(-jnp.expm1(-dt)),
        "gdn_norm": 1.0 + nrm(ks[11], (N_EVEN, GDN_DV), 0.02),
        "pool_w": nrm(ks[12], (N_EVEN, POOL_GROUPS, POOL_GROUP_W, POOL_GROUP_W), POOL_GROUP_W ** -0.5),
        "pool_scale": 1.0 + nrm(ks[13], (N_EVEN, POOL_W), 0.1),
        "od_w_in": nrm(ks[14], (N_ODD, D, ODD_IN), D ** -0.5),
        "od_w_out": nrm(ks[15], (N_ODD, ATT_W, D), ATT_W ** -0.5),
        "att_q_norm": 1.0 + nrm(ks[16], (N_ODD, ATT_DH), 0.02),
        "att_k_norm": 1.0 + nrm(ks[17], (N_ODD, ATT_DH), 0.02),
        "ffn_w_up": nrm(ks[18], (DEPTH, D, 2 * D_FF), D ** -0.5),
        "ffn_conv_w": nrm(ks[19], (DEPTH, FFN_CONV, D_FF), FFN_CONV ** -0.5),
        "ffn_conv_b": nrm(ks[20], (DEPTH, D_FF), 0.01),
        "ffn_w_down": nrm(ks[21], (DEPTH, D_FF, D), D_FF ** -0.5),
    }


def reference(x, c, ada_w, ada_b, norm_mix, norm_ffn, ev_w_in, ev_w_out, gdn_conv_w,
              gdn_a_log, gdn_dt_bias, gdn_norm, pool_w, pool_scale, od_w_in, od_w_out,
              att_q_norm, att_k_norm, ffn_w_up, ffn_conv_w, ffn_conv_b, ffn_w_down):
    cs = jax.nn.silu(c)
    for i in range(DEPTH):
        mod = (cs @ ada_w[i] + ada_b[i])[:, None, :]
        sh_m, sc_m, g_m, sh_f, sc_f, g_f = jnp.split(mod, 6, axis=-1)

        hm = rms_norm(x, norm_mix[i]) * (1.0 + sc_m) + sh_m
        if i % 2 == 0:
            e = i // 2
            y = even_mixer(hm, ev_w_in[e], ev_w_out[e], gdn_conv_w[e], gdn_a_log[e],
                           gdn_dt_bias[e], gdn_norm[e], pool_w[e], pool_scale[e])
        else:
            o = i // 2
            y = odd_mixer(hm, od_w_in[o], od_w_out[o], att_q_norm[o], att_k_norm[o])
        x = x + g_m * y

        hf = rms_norm(x, norm_ffn[i]) * (1.0 + sc_f) + sh_f
        x = x + g_f * conv_ffn(hf, ffn_w_up[i], ffn_conv_w[i], ffn_conv_b[i], ffn_w_down[i])
    return x
```

```python
import numpy as np
import ml_dtypes
from contextlib import ExitStack
import concourse.bass as bass
import concourse.mybir as mybir
from concourse.bass_utils import run_bass_kernel_spmd

F32 = mybir.dt.float32
BF16 = mybir.dt.bfloat16
AF = mybir.ActivationFunctionType
ALU = mybir.AluOpType

D = 1024
KC = 8
DFF = 2816
FC = 22
EPS = 1e-6
TT = 512


class Buf:
    __slots__ = ("name", "w", "r", "dch", "sch")

    def __init__(self, name):
        self.name = name
        self.w = None
        self.r = {}
        self.dch = None
        self.sch = None


class Chan:
    __slots__ = ("sem", "val", "kind")

    def __init__(self, sem, val):
        self.sem = sem
        self.val = val
        self.kind = None


class Eng:
    def __init__(self, name, handle, sem):
        self.name = name
        self.h = handle
        self.sem = sem
        self.cnt = 0
        self.waited = {}


class KB:
    def __init__(self, nc):
        self.nc = nc
        self.gstack = ExitStack()
        self.E = {}
        for name, h in (("pe", nc.tensor), ("act", nc.scalar), ("dve", nc.vector),
                        ("pool", nc.gpsimd), ("sp", nc.sync)):
            self.E[name] = Eng(name, h, self._newsem(name))
        self.free_ch = {}
        self.live_ch = []
        self.nsem = 5
        self.uid = 0

    def _newsem(self, name):
        self.semid = getattr(self, "semid", 0) + 1
        return self.gstack.enter_context(self.nc.semaphore(f"s_{name}_{self.semid}"))

    def chan(self, kind):
        fl = self.free_ch.setdefault(kind, [])
        while fl:
            c = fl.pop()
            if c.val < 24000:
                self.live_ch.append(c)
                return c
        self.nsem += 1
        c = Chan(self._newsem("ch" + kind), 0)
        c.kind = kind
        self.live_ch.append(c)
        return c

    def buf(self, name="b"):
        self.uid += 1
        return Buf(f"{name}{self.uid}")

    def bufs(self, n, name="b"):
        return [self.buf(name) for _ in range(n)]

    def _wait(self, e, dep):
        sem, val, src = dep
        if src == e.name and e.name in ("pe", "sp"):
            return
        k = id(sem)
        if e.waited.get(k, 0) >= val:
            return
        e.h.wait_ge(sem, val)
        e.waited[k] = val

    def _deps(self, e, reads, writes, dma_ch=None):
        for b in reads:
            if b.w is not None:
                self._wait(e, b.w)
        for b in writes:
            if b.w is not None:
                if not (dma_ch is not None and b.w[0] is dma_ch.sem):
                    self._wait(e, b.w)
            for dep in b.r.values():
                self._wait(e, dep)

    def op(self, eng, reads, writes, fn):
        e = self.E[eng]
        self._deps(e, reads, writes)
        ins = fn(e.h)
        ins.then_inc(e.sem, 1)
        e.cnt += 1
        tag = (e.sem, e.cnt, e.name)
        for b in writes:
            b.w = tag
            b.r = {}
        for b in reads:
            b.r[e.name] = tag

    def dma(self, q, out, in_, reads=(), writes=(), **kw):
        e = self.E[q]
        kind = "sw" if q == "pool" else "hw"
        if writes:
            b0 = writes[0]
            if b0.dch is None:
                b0.dch = self.chan(kind)
            ch = b0.dch
        else:
            b0 = reads[0]
            if b0.sch is None:
                b0.sch = self.chan(kind)
            ch = b0.sch
        assert ch.kind == kind
        for b in reads:
            if b.w is not None:
                self._wait(e, b.w)
        for b in writes:
            if b.w is not None and b.w[0] is not ch.sem:
                self._wait(e, b.w)
            for dep in b.r.values():
                self._wait(e, dep)
        e.h.dma_start(out=out, in_=in_, **kw).then_inc(ch.sem, 16)
        ch.val += 16
        tag = (ch.sem, ch.val, "dma")
        for b in writes:
            b.w = tag
            b.r = {}
        for b in reads:
            b.r[id(ch)] = tag

    def barrier(self):
        evs = []
        for e in self.E.values():
            if e.name != "sp" and e.cnt > 0:
                evs.append((e.sem, e.cnt, e.name))
        for c in self.live_ch:
            if c.val > 0:
                evs.append((c.sem, c.val, "dma"))
        for e in self.E.values():
            for ev in evs:
                if ev[2] == e.name:
                    continue
                self._wait(e, ev)
        for c in self.live_ch:
            self.free_ch.setdefault(c.kind, []).append(c)
        self.live_ch = []
        for e in self.E.values():
            if e.cnt > 8000:
                self.nsem += 1
                e.sem = self._newsem(e.name)
                e.cnt = 0

    def close(self):
        self.gstack.close()


def _pcol(v, n=None):
    v = np.asarray(v, np.float32).reshape(-1, 128)
    return np.ascontiguousarray(v.T)


class Packer:
    def __init__(self):
        self.cols = []
        self.off = {}
        self.n = 0

    def add(self, name, arr):
        arr = np.asarray(arr, np.float32)
        assert arr.shape[0] == 128
        arr = arr.reshape(128, -1)
        self.off[name] = (self.n, arr.shape[1])
        self.n += arr.shape[1]
        self.cols.append(arr)

    def pack(self):
        return np.ascontiguousarray(np.concatenate(self.cols, axis=1))


def pv_layout(depth):
    L = []
    L.append(("c", 8))
    for i in range(depth):
        L += [(f"nm{i}", 8), (f"nf{i}", 8), (f"adab{i}", 48), (f"fcw{i}", 66), (f"fcb{i}", 22)]
        if i % 2 == 0:
            L += [(f"gcw{i}", 48), (f"gnorm{i}", 1), (f"pscale{i}", 4), (f"alog{i}", 4), (f"dtb{i}", 4), (f"gnrow{i}", 128)]
        else:
            L += [(f"qn{i}", 1), (f"kn{i}", 1)]
    off = {}
    n = 0
    for k, w in L:
        off[k] = (n, w)
        n += w
    return off, n


def pack_pv(inp, b, depth):
    P = Packer()
    P.add("c", _pcol(inp["c"][b]))
    for i in range(depth):
        P.add(f"nm{i}", _pcol(inp["norm_mix"][i]))
        P.add(f"nf{i}", _pcol(inp["norm_ffn"][i]))
        P.add(f"adab{i}", _pcol(inp["ada_b"][i]))
        cw = inp["ffn_conv_w"][i]
        P.add(f"fcw{i}", np.concatenate([_pcol(cw[k]) for k in range(3)], axis=1))
        P.add(f"fcb{i}", _pcol(inp["ffn_conv_b"][i]))
        if i % 2 == 0:
            e = i // 2
            gw = inp["gdn_conv_w"][e]
            P.add(f"gcw{i}", np.concatenate([_pcol(gw[k]) for k in range(4)], axis=1))
            P.add(f"gnorm{i}", _pcol(inp["gdn_norm"][e]))
            P.add(f"pscale{i}", _pcol(inp["pool_scale"][e]))
            P.add(f"alog{i}", np.tile(np.asarray(inp["gdn_a_log"][e], np.float32)[None, :], (128, 1)))
            P.add(f"dtb{i}", np.tile(np.asarray(inp["gdn_dt_bias"][e], np.float32)[None, :], (128, 1)))
            P.add(f"gnrow{i}", np.tile(np.asarray(inp["gdn_norm"][e], np.float32)[None, :], (128, 1)))
        else:
            o = i // 2
            P.add(f"qn{i}", _pcol(inp["att_q_norm"][o]))
            P.add(f"kn{i}", _pcol(inp["att_k_norm"][o]))
    off, n = pv_layout(depth)
    assert P.off == off and P.n == n
    return P.pack()


def cst_layout(which=1):
    L = [("ident", 128), ("ones", 128), ("eps", 1), ("one", 1)]
    if which == 2:
        L = [("rc0", TT), ("rc1", TT), ("rc2", TT), ("rc3", TT),
             ("Ubd", 128), ("nUbd", 128), ("SLbd", 128), ("MnS", 128), ("MnIT", 128)]
    off = {}
    n = 0
    for k, w in L:
        off[k] = (n, w)
        n += w
    return off, n


def pack_cst():
    P = Packer()
    P.add("ident", np.eye(128, dtype=np.float32))
    P.add("ones", np.ones((128, 128), np.float32))
    P.add("eps", np.full((128, 1), EPS, np.float32))
    P.add("one", np.ones((128, 1), np.float32))
    off, n = cst_layout()
    assert P.off == off and P.n == n
    return P.pack()


def pack_cst2():
    P = Packer()
    t1 = np.arange(1, TT + 1, dtype=np.float32)
    for g, win in enumerate((2, 4, 8, 16)):
        P.add(f"rc{g}", np.tile((1.0 / np.minimum(t1, float(win)))[None, :], (128, 1)))
    i = np.arange(128)
    same = (i[:, None] // 64) == (i[None, :] // 64)
    Ubd = ((i[:, None] <= i[None, :]) & same).astype(np.float32)
    P.add("Ubd", Ubd)
    P.add("nUbd", -Ubd)
    P.add("SLbd", ((i[:, None] > i[None, :]) & same).astype(np.float32))
    P.add("MnS", np.where((i[:, None] > i[None, :]) & same, 0.0, -60000.0).astype(np.float32))
    P.add("MnIT", np.where((i[None, :] >= i[:, None]) & same, 0.0, -60000.0).astype(np.float32))
    off, n = cst_layout(2)
    assert P.off == off and P.n == n
    return P.pack()


class Prog:
    def __init__(self, T, depth, dbg=()):
        self.T = T
        self.depth = depth
        self.NT = T // TT
        self.dbg = set(dbg)
        nc = bass.Bass("TRN2", target_bir_lowering=False)
        self.nc = nc
        self.kb = KB(nc)
        self.pvoff, npv = pv_layout(depth)
        self.coff, ncst = cst_layout()
        ne = (depth + 1) // 2
        no = depth // 2

        def din(name, shape, dt=F32):
            return nc.dram_tensor(name, list(shape), dt, kind="ExternalInput").ap()

        self.x_in = din("x", [T, D])
        self.pv_in = din("pv", [128, npv])
        self.cst_in = din("cst", [128, ncst])
        self.coff2, ncst2 = cst_layout(2)
        self.cst2_in = din("cst2", [128, ncst2])
        self.ada_w = din("ada_w", [depth, D, 6 * D])
        self.ffn_w_up = din("ffn_w_up", [depth, D, 2 * DFF])
        self.ffn_w_down = din("ffn_w_down", [depth, DFF, D])
        self.y_out = nc.dram_tensor("y", [T, D], F32, kind="ExternalOutput").ap()
        self.xs = self.scratch("xs", [KC, 128, T], F32)
        self.act = self.scratch("act", [FC, 128, T], BF16)
        self.hs = self.scratch("hs", [KC, 128, T], BF16)
        self.qk = self.scratch("qk", [3, 16, 128, T], BF16)
        self.vv = self.scratch("vv", [3, T, D], BF16)
        self.attn = self.scratch("attn", [KC, 128, T], BF16)
        self.qkvT = self.scratch("qkvT", [12, 128, T], F32)
        self.kvtok = self.scratch("kvtok", [T, D], F32)
        self.ztok = self.scratch("ztok", [T, 512], F32)
        self.bgtok = self.scratch("bgtok", [T, 8], F32)
        self.oab = self.scratch("oab", [KC, 128, T], BF16)
        if ne > 0:
            self.ev_w_in = din("ev_w_in", [ne, D, 2568])
            self.ev_w_out = din("ev_w_out", [ne, D, D])
            self.pool_w = din("pool_w", [ne, 4, 128, 128])
        if no > 0:
            self.od_w_in = din("od_w_in", [no, D, 9 * D])
            self.od_w_out = din("od_w_out", [no, D, D])
            self.alibi = din("alibi", [128, 3 * 8 * 4 * 128], BF16)

    def sb(self, st, name, shape, dt):
        self.kb.uid += 1
        return st.enter_context(self.nc.sbuf_tensor(f"{name}_{self.kb.uid}", list(shape), dt))

    def pst(self, st, name, shape, dt):
        self.kb.uid += 1
        return st.enter_context(self.nc.psum_tensor(f"{name}_{self.kb.uid}", list(shape), dt))

    def scratch(self, name, shape, dt):
        kind = "ExternalOutput" if name in self.dbg else "Internal"
        return self.nc.dram_tensor(name, list(shape), dt, kind=kind).ap()

    def setup_globals(self, st):
        nc, kb = self.nc, self.kb
        _, npv = pv_layout(self.depth)
        _, ncst = cst_layout()
        self.pv = self.sb(st, "pv", [128, npv], F32)
        self.cst = self.sb(st, "cst", [128, ncst], F32)
        self.mod = self.sb(st, "mod", [128, self.depth, 64], F32)
        self.b_pv, self.b_cst, self.b_mod = kb.buf("pv"), kb.buf("cst"), kb.buf("mod")
        kb.dma("sp", self.pv[:], self.pv_in[:, :], writes=[self.b_pv])
        kb.dma("sp", self.cst[:], self.cst_in[:, :], writes=[self.b_cst])
        self.ps = []
        self.b_ps = []
        for i in range(8):
            self.ps.append(self.pst(st, f"ps{i}", [128, 512], F32))
            self.b_ps.append(kb.buf("ps"))

    def P(self, name, lo=0, hi=None):
        o, w = self.pvoff[name]
        hi = w if hi is None else hi
        return self.pv[:, o + lo:o + hi]

    def load_c2(self, st, first, last):
        o0 = self.coff2[first][0]
        o1 = self.coff2[last][0] + self.coff2[last][1]
        t = self.sb(st, "c2", [128, o1 - o0], F32)
        b = self.kb.buf("c2")
        self.kb.dma("sp", t[:], self.cst2_in[:, o0:o1], writes=[b])

        def acc(name):
            o, w = self.coff2[name]
            return t[:, o - o0:o - o0 + w]
        return acc, b

    def C(self, name, lo=0, hi=None):
        o, w = self.coff[name]
        hi = w if hi is None else hi
        return self.cst[:, o + lo:o + hi]

    def M(self, i, which, kc):
        base = {"sh_m": 0, "sc_m": 8, "g_m": 16, "sh_f": 24, "sc_f": 32, "g_f": 40,
                "gs_m": 48, "gs_f": 56}[which]
        return self.mod[:, i, base + kc:base + kc + 1]

    def phase_mod(self):
        nc, kb = self.nc, self.kb
        with ExitStack() as st:
            cs = self.sb(st, "cs", [128, 8], F32)
            b_cs = kb.buf("cs")
            wst = self.sb(st, "adaw", [128, 2, 8, 512], F32)
            b_w = kb.bufs(2, "adaw")
            kb.op("act", [self.b_pv], [b_cs],
                  lambda e: e.activation(out=cs[:], in_=self.P("c"), func=AF.Silu))
            n = 0
            for i in range(self.depth):
                for g in range(12):
                    s = n % 2
                    n += 1
                    src = self.ada_w[i, :, g * 512:(g + 1) * 512].rearrange("(k p) c -> p k c", p=128)
                    kb.dma("sp", wst[:, s], src, writes=[b_w[s]])
                    pb = self.b_ps[s]
                    pt = self.ps[s]
                    for cc in range(4):
                        for k in range(8):
                            kb.op("pe", [b_w[s], b_cs], [pb],
                                  lambda e, cc=cc, k=k: e.matmul(
                                      pt[:, cc:cc + 1], lhsT=wst[:, s, k, cc * 128:(cc + 1) * 128],
                                      rhs=cs[:, k:k + 1], start=(k == 0), stop=(k == 7)))
                    o, _ = self.pvoff[f"adab{i}"]
                    kb.op("dve", [pb, self.b_pv], [self.b_mod],
                          lambda e, g=g, o=o, i=i: e.tensor_tensor(
                              out=self.mod[:, i, g * 4:(g + 1) * 4], in0=pt[:, 0:4],
                              in1=self.pv[:, o + g * 4:o + (g + 1) * 4], op=ALU.add))
                for nm, sc, dst in ((f"nm{i}", 8, 48), (f"nf{i}", 32, 56)):
                    kb.op("dve", [self.b_mod, self.b_pv], [self.b_mod],
                          lambda e, nm=nm, sc=sc, dst=dst, i=i: e.scalar_tensor_tensor(
                              out=self.mod[:, i, dst:dst + 8], in0=self.mod[:, i, sc:sc + 8],
                              scalar=1.0, in1=self.P(nm), op0=ALU.add, op1=ALU.mult))
            kb.barrier()

    def phase_in(self):
        nc, kb = self.nc, self.kb
        with ExitStack() as st:
            xin = self.sb(st, "xin", [128, 2, 4, D], F32)
            xo = self.sb(st, "xo", [128, 2, KC, TT], F32)
            b_in, b_o = kb.bufs(2, "xin"), kb.bufs(2, "xo")
            for j in range(self.NT):
                s = j % 2
                kb.dma("sp", xin[:, s], self.x_in[j * TT:(j + 1) * TT, :].rearrange("(a p) d -> p a d", p=128),
                       writes=[b_in[s]])
                for kc in range(KC):
                    pi = kc % 4
                    for a in range(4):
                        kb.op("pe", [b_in[s], self.b_cst], [self.b_ps[pi]],
                              lambda e, a=a, kc=kc, pi=pi: e.transpose(
                                  self.ps[pi][:, a * 128:(a + 1) * 128],
                                  xin[:, s, a, kc * 128:(kc + 1) * 128], self.C("ident")))
                    if kc % 2 == 0:
                        kb.op("dve", [self.b_ps[pi]], [b_o[s]],
                              lambda e, kc=kc, pi=pi: e.tensor_copy(out=xo[:, s, kc, :], in_=self.ps[pi][:]))
                    else:
                        kb.op("act", [self.b_ps[pi]], [b_o[s]],
                              lambda e, kc=kc, pi=pi: e.copy(out=xo[:, s, kc, :], in_=self.ps[pi][:]))
                kb.dma("pool", self.xs[:, :, j * TT:(j + 1) * TT].rearrange("k p t -> p k t"), xo[:, s],
                       reads=[b_o[s]])
            kb.barrier()

    def phase_out(self):
        nc, kb = self.nc, self.kb
        with ExitStack() as st:
            X = self.sb(st, "X", [128, 2, KC, TT], F32)
            yo = self.sb(st, "yo", [128, 2, 4, D], F32)
            b_X, b_y = kb.bufs(2, "X"), kb.bufs(2, "yo")
            n = 0
            for j in range(self.NT):
                s = j % 2
                kb.dma("sp", X[:, s], self.xs[:, :, j * TT:(j + 1) * TT].rearrange("k p t -> p k t"),
                       writes=[b_X[s]])
                for a in range(4):
                    for hf in range(2):
                        pi = n % 4
                        n += 1
                        for q in range(4):
                            kc = hf * 4 + q
                            kb.op("pe", [b_X[s], self.b_cst], [self.b_ps[pi]],
                                  lambda e, a=a, kc=kc, q=q, pi=pi: e.transpose(
                                      self.ps[pi][:, q * 128:(q + 1) * 128],
                                      X[:, s, kc, a * 128:(a + 1) * 128], self.C("ident")))
                        if n % 2 == 0:
                            kb.op("dve", [self.b_ps[pi]], [b_y[s]],
                                  lambda e, a=a, hf=hf, pi=pi: e.tensor_copy(
                                      out=yo[:, s, a, hf * 512:(hf + 1) * 512], in_=self.ps[pi][:]))
                        else:
                            kb.op("act", [self.b_ps[pi]], [b_y[s]],
                                  lambda e, a=a, hf=hf, pi=pi: e.copy(
                                      out=yo[:, s, a, hf * 512:(hf + 1) * 512], in_=self.ps[pi][:]))
                kb.dma("pool", self.y_out[j * TT:(j + 1) * TT, :].rearrange("(a p) d -> p a d", p=128), yo[:, s],
                       reads=[b_y[s]])
            kb.barrier()

    def load_w(self, st, name, Wd, nk, C, SC=1408):
        nc, kb = self.nc, self.kb
        W = self.sb(st, name, [128, nk, C], BF16)
        bW = kb.buf(name)
        stg = self.sb(st, name + "_stg", [128, 2, SC], F32)
        b_s = kb.bufs(2, "stg")
        n = 0
        for k in range(nk):
            for c0 in range(0, C, SC):
                c1 = min(C, c0 + SC)
                s = n % 2
                kb.dma("sp", stg[:, s, 0:c1 - c0], Wd[k * 128:(k + 1) * 128, c0:c1], writes=[b_s[s]])
                eng = ("dve", "act", "pool")[n % 3]
                if eng == "act":
                    kb.op("act", [b_s[s]], [bW], lambda e, k=k, c0=c0, c1=c1, s=s: e.copy(
                        out=W[:, k, c0:c1], in_=stg[:, s, 0:c1 - c0]))
                else:
                    kb.op(eng, [b_s[s]], [bW], lambda e, k=k, c0=c0, c1=c1, s=s: e.tensor_copy(
                        out=W[:, k, c0:c1], in_=stg[:, s, 0:c1 - c0]))
                n += 1
        return W, bW

    def norm_alloc(self, st):
        nc, kb = self.nc, self.kb
        r = {}
        r["sq"] = self.sb(st, "sq", [128, 2, TT], F32)
        r["b_sq"] = kb.bufs(2, "sq")
        r["rs"] = self.sb(st, "rs", [128, 2, TT], F32)
        r["b_rs"] = kb.bufs(2, "rs")
        r["t"] = self.sb(st, "nt", [128, 2, TT], F32)
        r["b_t"] = kb.bufs(2, "nt")
        r["n"] = 0
        return r

    def norm_tile(self, r, X, bX, h, bh, li, which, psi):
        kb = self.kb
        pb, pt = self.b_ps[psi], self.ps[psi]
        gsn, shn = ("gs_m", "sh_m") if which == "m" else ("gs_f", "sh_f")
        for kc in range(KC):
            s = r["n"] % 2
            r["n"] += 1
            kb.op("act", [bX], [r["b_sq"][s]],
                  lambda e, kc=kc, s=s: e.activation(out=r["sq"][:, s, :], in_=X[:, kc, :], func=AF.Square))
            kb.op("pe", [r["b_sq"][s], self.b_cst], [pb],
                  lambda e, kc=kc, s=s: e.matmul(pt[:], lhsT=self.C("ones"), rhs=r["sq"][:, s, :],
                                                 start=(kc == 0), stop=(kc == KC - 1)))
        kb.op("act", [pb], [r["b_rs"][0]],
              lambda e: e.activation(out=r["rs"][:, 0, :], in_=pt[:], func=AF.Ln, bias=self.C("eps"),
                                     scale=1.0 / D))
        kb.op("act", [r["b_rs"][0]], [r["b_rs"][1]],
              lambda e: e.activation(out=r["rs"][:, 1, :], in_=r["rs"][:, 0, :], func=AF.Exp, scale=-0.5))
        for kc in range(KC):
            s = r["n"] % 2
            r["n"] += 1
            kb.op("dve", [bX, r["b_rs"][1], self.b_mod], [r["b_t"][s]],
                  lambda e, kc=kc, s=s: e.scalar_tensor_tensor(
                      out=r["t"][:, s, :], in0=X[:, kc, :], scalar=self.M(li, gsn, kc),
                      in1=r["rs"][:, 1, :], op0=ALU.mult, op1=ALU.mult))
            kb.op("act", [r["b_t"][s], self.b_mod], [bh],
                  lambda e, kc=kc, s=s: e.activation(
                      out=h[:, kc, :], in_=r["t"][:, s, :], func=AF.Identity, bias=self.M(li, shn, kc), scale=1.0))

    def phase_ffn_up(self, li, xs):
        nc, kb = self.nc, self.kb
        NT = self.NT
        with ExitStack() as st:
            W, bW = self.load_w(st, "wup", self.ffn_w_up[li], KC, 2 * DFF)
            X = self.sb(st, "X", [128, KC, TT], F32)
            bX = kb.buf("X")
            h = self.sb(st, "h", [128, 2, KC, TT], BF16)
            bh = kb.bufs(2, "h")
            A = self.sb(st, "A", [128, 2, FC, TT], BF16)
            bA = kb.bufs(2, "A")
            G = self.sb(st, "G", [128, 2, TT + 2], F32)
            bG = kb.bufs(2, "G")
            acc = self.sb(st, "acc", [128, 2, TT], F32)
            bacc = kb.bufs(2, "acc")
            sg = self.sb(st, "sg", [128, 2, TT], F32)
            bsg = kb.bufs(2, "sg")
            H = self.sb(st, "H", [128, FC, 2], F32)
            bH = kb.buf("H")
            nr = self.norm_alloc(st)
            kb.op("pool", [], [bH], lambda e: e.memset(H[:], 0.0))
            cwo, _ = self.pvoff[f"fcw{li}"]
            cbo, _ = self.pvoff[f"fcb{li}"]

            def load_x(j):
                kb.dma("sp", X[:], xs[:, :, j * TT:(j + 1) * TT].rearrange("k p t -> p k t"), writes=[bX])

            load_x(0)
            self.norm_tile(nr, X, bX, h[:, 0], bh[0], li, "f", 4)
            n = 0
            for j in range(NT):
                s = j % 2
                if j + 1 < NT:
                    load_x(j + 1)
                for c in range(FC):
                    u = n % 2
                    n += 1
                    pg, pv_ = self.ps[u], self.ps[2 + u]
                    bpg, bpv = self.b_ps[u], self.b_ps[2 + u]
                    for k in range(KC):
                        kb.op("pe", [bW, bh[s]], [bpg], lambda e, k=k, c=c: e.matmul(
                            pg[:], lhsT=W[:, k, c * 128:(c + 1) * 128], rhs=h[:, s, k, :],
                            start=(k == 0), stop=(k == KC - 1)))
                    for k in range(KC):
                        kb.op("pe", [bW, bh[s]], [bpv], lambda e, k=k, c=c: e.matmul(
                            pv_[:], lhsT=W[:, k, DFF + c * 128:DFF + (c + 1) * 128], rhs=h[:, s, k, :],
                            start=(k == 0), stop=(k == KC - 1)))
                    kb.op("act", [bH], [bG[u]], lambda e, c=c, u=u: e.copy(
                        out=G[:, u, 0:2], in_=H[:, c, :]))
                    kb.op("act", [bpg], [bG[u]], lambda e, u=u: e.copy(out=G[:, u, 2:TT + 2], in_=pg[:]))
                    kb.op("act", [bpg], [bH], lambda e, c=c: e.copy(
                        out=H[:, c, :], in_=pg[:, TT - 2:TT]))
                    kb.op("act", [bpg, self.b_pv], [bacc[u]], lambda e, c=c, u=u: e.activation(
                        out=acc[:, u, :], in_=pg[:], func=AF.Identity,
                        scale=self.pv[:, cwo + 44 + c:cwo + 44 + c + 1],
                        bias=self.pv[:, cbo + c:cbo + c + 1]))
                    kb.op("dve", [bG[u], bacc[u], self.b_pv], [bacc[u]], lambda e, c=c, u=u: e.scalar_tensor_tensor(
                        out=acc[:, u, :], in0=G[:, u, 1:TT + 1], scalar=self.pv[:, cwo + 22 + c:cwo + 22 + c + 1],
                        in1=acc[:, u, :], op0=ALU.mult, op1=ALU.add))
                    kb.op("dve", [bG[u], bacc[u], self.b_pv], [bacc[u]], lambda e, c=c, u=u: e.scalar_tensor_tensor(
                        out=acc[:, u, :], in0=G[:, u, 0:TT], scalar=self.pv[:, cwo + c:cwo + c + 1],
                        in1=acc[:, u, :], op0=ALU.mult, op1=ALU.add))
                    kb.op("act", [bacc[u]], [bsg[u]], lambda e, u=u: e.activation(
                        out=sg[:, u, :], in_=acc[:, u, :], func=AF.Silu))
                    kb.op("dve", [bsg[u], bpv], [bA[s]], lambda e, c=c, u=u: e.tensor_tensor(
                        out=A[:, s, c, :], in0=sg[:, u, :], in1=pv_[:], op=ALU.mult))
                    if c == FC // 2 and j + 1 < NT:
                        self.norm_tile(nr, X, bX, h[:, 1 - s], bh[1 - s], li, "f", 4)
                kb.dma("pool", self.act[:, :, j * TT:(j + 1) * TT].rearrange("k p t -> p k t"), A[:, s],
                       reads=[bA[s]])
            kb.barrier()

    def phase_proj_res(self, name, src, nk, Wd, li, gname, xs_in, xs_out):
        nc, kb = self.nc, self.kb
        NT = self.NT
        with ExitStack() as st:
            W, bW = self.load_w(st, "w" + name, Wd, nk, D, SC=1024)
            X = self.sb(st, "X", [128, 2, KC, TT], F32)
            bX = kb.bufs(2, "X")
            A = self.sb(st, "A", [128, 2, nk, TT], BF16)
            bA = kb.bufs(2, "A")

            def load(j):
                s = j % 2
                kb.dma("sp", A[:, s], src[:, :, j * TT:(j + 1) * TT].rearrange("k p t -> p k t"), writes=[bA[s]])
                kb.dma("sp", X[:, s], xs_in[:, :, j * TT:(j + 1) * TT].rearrange("k p t -> p k t"), writes=[bX[s]])

            load(0)
            n = 0
            for j in range(NT):
                s = j % 2
                if j + 1 < NT:
                    load(j + 1)
                for oc in range(KC):
                    u = n % 2
                    n += 1
                    pt, pb = self.ps[u], self.b_ps[u]
                    for k in range(nk):
                        kb.op("pe", [bW, bA[s]], [pb], lambda e, k=k, oc=oc: e.matmul(
                            pt[:], lhsT=W[:, k, oc * 128:(oc + 1) * 128], rhs=A[:, s, k, :],
                            start=(k == 0), stop=(k == nk - 1)))
                    kb.op("dve", [pb, bX[s], self.b_mod], [bX[s]], lambda e, oc=oc: e.scalar_tensor_tensor(
                        out=X[:, s, oc, :], in0=pt[:], scalar=self.M(li, gname, oc), in1=X[:, s, oc, :],
                        op0=ALU.mult, op1=ALU.add))
                kb.dma("pool", xs_out[:, :, j * TT:(j + 1) * TT].rearrange("k p t -> p k t"), X[:, s],
                       reads=[bX[s]])
            kb.barrier()

    def phase_norm(self, li, which, xs):
        nc, kb = self.nc, self.kb
        with ExitStack() as st:
            X = self.sb(st, "X", [128, 2, KC, TT], F32)
            bX = kb.bufs(2, "X")
            h = self.sb(st, "h", [128, 2, KC, TT], BF16)
            bh = kb.bufs(2, "h")
            nr = self.norm_alloc(st)
            for j in range(self.NT):
                s = j % 2
                kb.dma("sp", X[:, s], xs[:, :, j * TT:(j + 1) * TT].rearrange("k p t -> p k t"), writes=[bX[s]])
                self.norm_tile(nr, X[:, s], bX[s], h[:, s], bh[s], li, which, 4 + s)
                kb.dma("pool", self.hs[:, :, j * TT:(j + 1) * TT].rearrange("k p t -> p k t"), h[:, s],
                       reads=[bh[s]])
            kb.barrier()

    def phase_odd_proj(self, li, g):
        nc, kb = self.nc, self.kb
        o = li // 2
        with ExitStack() as st:
            W, bW = self.load_w(st, "wod", self.od_w_in[o][:, g * 3072:(g + 1) * 3072], KC, 3072, SC=1024)
            h = self.sb(st, "h", [128, 2, KC, TT], BF16)
            bh = kb.bufs(2, "h")
            QK = self.sb(st, "QK", [128, 2, 16, TT], BF16)
            bQK = kb.bufs(2, "QK")
            Vt = self.sb(st, "Vt", [128, 2, 4, D], BF16)
            bVt = kb.bufs(2, "Vt")
            sq = self.sb(st, "sq", [128, 4, TT], BF16)
            bsq = kb.bufs(4, "sq")
            rs = self.sb(st, "rs", [128, 2, TT], F32)
            brs = kb.bufs(2, "rs")
            ri = self.sb(st, "ri", [128, 2, TT], F32)
            bri = kb.bufs(2, "ri")
            onesb = self.sb(st, "onesb", [128, 128], BF16)
            bonesb = kb.buf("onesb")
            kb.op("dve", [self.b_cst], [bonesb], lambda e: e.tensor_copy(out=onesb[:], in_=self.C("ones")))
            gn = self.sb(st, "gn", [128, 2], F32)
            bgn = kb.buf("gn")
            kb.op("pool", [self.b_pv], [bgn], lambda e: e.tensor_scalar(
                out=gn[:, 0:1], in0=self.P(f"qn{li}"), scalar1=128.0 ** -0.5, scalar2=None, op0=ALU.mult))
            kb.op("pool", [self.b_pv], [bgn], lambda e: e.tensor_copy(out=gn[:, 1:2], in_=self.P(f"kn{li}")))

            def load(j):
                s = j % 2
                kb.dma("sp", h[:, s], self.hs[:, :, j * TT:(j + 1) * TT].rearrange("k p t -> p k t"), writes=[bh[s]])

            load(0)
            n = 0
            nv = 0
            for j in range(self.NT):
                s = j % 2
                if j + 1 < self.NT:
                    load(j + 1)
                def partA(c, u):
                    pp, bpp = self.ps[u], self.b_ps[u]
                    for k in range(KC):
                        kb.op("pe", [bW, bh[s]], [bpp], lambda e, k=k: e.matmul(
                            pp[:], lhsT=W[:, k, c * 128:(c + 1) * 128], rhs=h[:, s, k, :],
                            start=(k == 0), stop=(k == KC - 1)))
                    kb.op("act", [bpp], [bsq[u]], lambda e: e.activation(
                        out=sq[:, u, :], in_=pp[:], func=AF.Square))

                def partB(c, u):
                    pp, bpp = self.ps[u], self.b_ps[u]
                    um = u % 2
                    pm, bpm = self.ps[4 + um], self.b_ps[4 + um]
                    kb.op("pe", [bsq[u], bonesb], [bpm], lambda e: e.matmul(
                        pm[:], lhsT=onesb[:], rhs=sq[:, u, :], start=True, stop=True))
                    kb.op("act", [bpm], [brs[um]], lambda e: e.activation(
                        out=rs[:, um, :], in_=pm[:], func=AF.Ln, bias=self.C("eps"), scale=1.0 / 128))
                    kb.op("act", [brs[um]], [bri[um]], lambda e: e.activation(
                        out=ri[:, um, :], in_=rs[:, um, :], func=AF.Exp, scale=-0.5))
                    kb.op("dve", [bpp, bri[um], bgn], [bQK[s]], lambda e: e.scalar_tensor_tensor(
                        out=QK[:, s, c, :], in0=pp[:], scalar=gn[:, c // 8:c // 8 + 1], in1=ri[:, um, :],
                        op0=ALU.mult, op1=ALU.mult))

                for c in range(16):
                    partA(c, (n + c) % 4)
                    if c > 0:
                        partB(c - 1, (n + c - 1) % 4)
                partB(15, (n + 15) % 4)
                n += 16
                kb.dma("pool", self.qk[g, :, :, j * TT:(j + 1) * TT].rearrange("k p t -> p k t"), QK[:, s],
                       reads=[bQK[s]])
                for a in range(4):
                    for hf in range(2):
                        u = 6 + nv % 2
                        nv += 1
                        pp, bpp = self.ps[u], self.b_ps[u]
                        for k in range(KC):
                            kb.op("pe", [bW, bh[s]], [bpp], lambda e, k=k, a=a, hf=hf: e.matmul(
                                pp[:], lhsT=h[:, s, k, a * 128:(a + 1) * 128],
                                rhs=W[:, k, 2048 + hf * 512:2048 + (hf + 1) * 512],
                                start=(k == 0), stop=(k == KC - 1)))
                        kb.op("act", [bpp], [bVt[s]], lambda e, a=a, hf=hf: e.copy(
                            out=Vt[:, s, a, hf * 512:(hf + 1) * 512], in_=pp[:]))
                kb.dma("pool", self.vv[g, j * TT:(j + 1) * TT, :].rearrange("(a p) c -> p a c", p=128), Vt[:, s],
                       reads=[bVt[s]])
            kb.barrier()

    def phase_attn(self, li):
        nc, kb = self.nc, self.kb
        T = self.T
        with ExitStack() as st:
            q = self.sb(st, "q", [128, 2, T], BF16)
            k_ = self.sb(st, "k", [128, 2, T], BF16)
            V = self.sb(st, "V", [128, 2, T // 128, 128], BF16)
            bq, bk, bV = kb.bufs(2, "q"), kb.bufs(2, "k"), kb.bufs(2, "V")
            ab = self.sb(st, "ab", [128, 2, 3, 4, 128], BF16)
            bab = kb.bufs(2, "ab")
            acc = self.sb(st, "acc", [128, 2, T], F32)
            bacc = kb.buf("acc")
            oo = self.sb(st, "oo", [128, T], BF16)
            boo = kb.buf("oo")
            PT = self.sb(st, "PT", [128, 4, 256], BF16)
            bPT = kb.bufs(4, "PT")
            cb = self.sb(st, "cb", [128, 256], BF16)
            bcb = kb.buf("cb")
            kb.op("dve", [self.b_cst], [bcb], lambda e: e.tensor_copy(out=cb[:, 0:128], in_=self.C("ident")))
            kb.op("dve", [self.b_cst], [bcb], lambda e: e.tensor_copy(out=cb[:, 128:256], in_=self.C("ones")))
            it = 0
            nu = 0
            for hd in range(8):
                hs_ = hd % 2
                kb.dma("sp", ab[:, hs_], self.alibi.rearrange("p (g h f) -> p g h f", g=3, h=8)[:, :, hd, :]
                       .rearrange("p g (i a) -> p g i a", i=4), writes=[bab[hs_]])
                for g, dil in enumerate((1, 4, 16)):
                    s = it % 2
                    it += 1
                    kb.dma("sp", q[:, s, :], self.qk[g, hd, :, :], writes=[bq[s]])
                    kb.dma("sp", k_[:, s, :], self.qk[g, 8 + hd, :, :], writes=[bk[s]])
                    vsrc = self.vv[g, :, hd * 128:(hd + 1) * 128].rearrange("(n p r) c -> p r n c", p=128, r=dil)
                    nb = T // dil // 128
                    for r in range(dil):
                        for n0 in range(0, nb, 16):
                            n1 = min(nb, n0 + 16)
                            kb.dma("sp", V[:, s, r * nb + n0:r * nb + n1, :], vsrc[:, r, n0:n1, :], writes=[bV[s]])
                    units = [(r, n) for r in range(dil) for n in range(nb)]

                    def sl(r, nn):
                        st0 = r + dil * 128 * nn
                        return slice(st0, st0 + dil * 127 + 1, dil)

                    def emit_S(idx, u):
                        r, n = units[idx]
                        pS, bpS = self.ps[u], self.b_ps[u]
                        lo = 0 if n > 0 else 128
                        for bi, nn in ((0, n - 1), (1, n)):
                            if nn < 0:
                                continue
                            cols = slice(bi * 128, (bi + 1) * 128)
                            kb.op("pe", [bk[s], bq[s]], [bpS], lambda e: e.matmul(
                                pS[:, cols], lhsT=k_[:, s, sl(r, nn)], rhs=q[:, s, sl(r, n)], start=True, stop=False))
                            kb.op("pe", [bcb, bab[hs_]], [bpS], lambda e: e.matmul(
                                pS[:, cols], lhsT=cb[:, 0:128], rhs=ab[:, hs_, g, 2 * bi, :], start=False, stop=False))
                            kb.op("pe", [bcb, bab[hs_]], [bpS], lambda e: e.matmul(
                                pS[:, cols], lhsT=cb[:, 0:128], rhs=ab[:, hs_, g, 2 * bi + 1, :], start=False, stop=True))
                        kb.op("act", [bpS], [bPT[u]], lambda e: e.activation(
                            out=PT[:, u, lo:256], in_=pS[:, lo:256], func=AF.Exp))

                    def emit_PV(idx, u):
                        r, n = units[idx]
                        pO, bpO = self.ps[4 + u], self.b_ps[4 + u]
                        blks = [(bi, nn) for bi, nn in ((0, n - 1), (1, n)) if nn >= 0]
                        for half, is_l in ((0, False), (1, True)):
                            for ii, (bi, nn) in enumerate(blks):
                                lhs = cb[:, 128:256] if is_l else V[:, s, r * nb + nn, :]
                                kb.op("pe", [bV[s], bcb, bPT[u]], [bpO], lambda e: e.matmul(
                                    pO[:, half * 128:(half + 1) * 128], lhsT=lhs, rhs=PT[:, u, bi * 128:(bi + 1) * 128],
                                    start=(ii == 0), stop=(ii == len(blks) - 1)))
                        dst = acc[:, :, sl(r, n)]
                        src = pO[:, 0:256].rearrange("p (h a) -> p h a", h=2)
                        if g == 0:
                            kb.op("act", [bpO], [bacc], lambda e: e.copy(out=dst, in_=src))
                        else:
                            kb.op("dve", [bpO, bacc], [bacc], lambda e: e.tensor_tensor(
                                out=dst, in0=src, in1=dst, op=ALU.add))

                    LA = 2
                    base = nu
                    for i in range(min(LA, len(units))):
                        emit_S(i, (base + i) % 4)
                    for i in range(len(units)):
                        if i + LA < len(units):
                            emit_S(i + LA, (base + i + LA) % 4)
                        emit_PV(i, (base + i) % 4)
                    nu += len(units)
                kb.op("act", [bacc], [bacc], lambda e: e.activation(out=acc[:, 1, :], in_=acc[:, 1, :], func=AF.Ln))
                kb.op("act", [bacc], [bacc], lambda e: e.activation(out=acc[:, 1, :], in_=acc[:, 1, :], func=AF.Exp, scale=-1.0))
                kb.op("dve", [bacc], [boo], lambda e: e.tensor_tensor(
                    out=oo[:], in0=acc[:, 0, :], in1=acc[:, 1, :], op=ALU.mult))
                kb.dma("pool", self.attn[hd, :, :], oo[:], reads=[boo])
            kb.barrier()

    def phase_even_proj(self, li):
        nc, kb = self.nc, self.kb
        e_ = li // 2
        NT = self.NT
        with ExitStack() as st:
            W, bW = self.load_w(st, "wev", self.ev_w_in[e_], KC, 2568, SC=1284)
            PW = self.sb(st, "PW", [128, 4, 128], BF16)
            bPW = kb.buf("PW")
            pws = self.sb(st, "pws", [128, 4, 128], F32)
            bpws = kb.buf("pws")
            kb.dma("sp", pws[:], self.pool_w[e_].rearrange("g c d -> c g d"), writes=[bpws])
            kb.op("dve", [bpws], [bPW], lambda e: e.tensor_copy(out=PW[:], in_=pws[:]))
            h = self.sb(st, "h", [128, 2, KC, TT], BF16)
            bh = kb.bufs(2, "h")
            QKV = self.sb(st, "QKV", [128, 1, 12, TT], F32)
            bQKV = kb.bufs(1, "QKV") * 2
            KVt = self.sb(st, "KVt", [128, 1, 4, D], F32)
            bKVt = kb.bufs(1, "KVt") * 2
            Zt = self.sb(st, "Zt", [128, 1, 4, 512], F32)
            bZt = kb.bufs(1, "Zt") * 2
            BG = self.sb(st, "BG", [128, 2, 4, 8], F32)
            bBG = kb.bufs(2, "BG")
            OB = self.sb(st, "OB", [128, 2, 4, TT], BF16)
            bOB = kb.bufs(2, "OB")
            G = self.sb(st, "G", [128, 4, TT + 3], F32)
            bG = kb.bufs(4, "G")
            acc = self.sb(st, "acc", [128, 4, TT], F32)
            bacc = kb.bufs(4, "acc")
            sg = self.sb(st, "sg", [128, 4, TT], F32)
            bsg = kb.bufs(4, "sg")
            sq = self.sb(st, "sq", [128, 4, TT], BF16)
            bsq = kb.bufs(4, "sq")
            rs = self.sb(st, "rs", [128, 4, TT], F32)
            brs = kb.bufs(4, "rs")
            Hc = self.sb(st, "Hc", [128, 12, 3], F32)
            bHc = kb.buf("Hc")
            PB = self.sb(st, "PB", [128, 2, TT + 15], F32)
            bPB = kb.bufs(2, "PB")
            S2 = self.sb(st, "S2", [128, 2, TT + 15], F32)
            bS2 = kb.bufs(2, "S2")
            S4 = self.sb(st, "S4", [128, 2, TT + 15], F32)
            bS4 = kb.bufs(2, "S4")
            Hp = self.sb(st, "Hp", [128, 4, 15], F32)
            bHp = kb.buf("Hp")
            pl = self.sb(st, "pl", [128, 2, TT], BF16)
            bpl = kb.bufs(2, "pl")
            bt = self.sb(st, "bt", [128, 4, 8], F32)
            bbt = kb.bufs(4, "bt")
            ea = self.sb(st, "ea", [128, 4], F32)
            bea = kb.buf("ea")
            onesb = self.sb(st, "onesb", [128, 128], BF16)
            bonesb = kb.buf("onesb")
            kb.op("dve", [self.b_cst], [bonesb], lambda e: e.tensor_copy(out=onesb[:], in_=self.C("ones")))
            kb.op("pool", [], [bHc], lambda e: e.memset(Hc[:], 0.0))
            kb.op("pool", [], [bHp], lambda e: e.memset(Hp[:], 0.0))
            kb.op("pool", [], bS2, lambda e: e.memset(S2[:], 0.0))
            kb.op("pool", [], bS4, lambda e: e.memset(S4[:], 0.0))
            kb.op("act", [self.b_pv], [bea], lambda e: e.activation(out=ea[:], in_=self.P(f"alog{li}"), func=AF.Exp))
            gco, _ = self.pvoff[f"gcw{li}"]
            pso, _ = self.pvoff[f"pscale{li}"]
            C2, bc2 = self.load_c2(st, "rc0", "rc3")

            def load(j):
                s = j % 2
                kb.dma("sp", h[:, s], self.hs[:, :, j * TT:(j + 1) * TT].rearrange("k p t -> p k t"), writes=[bh[s]])

            load(0)
            n = 0
            for j in range(NT):
                s = j % 2
                if j + 1 < NT:
                    load(j + 1)
                def partA(c, u):
                    pp, bpp = self.ps[u], self.b_ps[u]
                    for k in range(KC):
                        kb.op("pe", [bW, bh[s]], [bpp], lambda e, k=k: e.matmul(
                            pp[:], lhsT=W[:, k, c * 128:(c + 1) * 128], rhs=h[:, s, k, :],
                            start=(k == 0), stop=(k == KC - 1)))
                    kb.op("act", [bHc], [bG[u]], lambda e: e.copy(out=G[:, u, 0:3], in_=Hc[:, c, :]))
                    kb.op("act", [bpp], [bG[u]], lambda e: e.copy(out=G[:, u, 3:TT + 3], in_=pp[:]))
                    kb.op("act", [bpp], [bHc], lambda e: e.copy(out=Hc[:, c, :], in_=pp[:, TT - 3:TT]))
                    kb.op("act", [bpp, self.b_pv], [bacc[u]], lambda e: e.activation(
                        out=acc[:, u, :], in_=pp[:], func=AF.Copy, scale=self.pv[:, gco + 36 + c:gco + 36 + c + 1]))
                    for kk in range(3):
                        kb.op("dve", [bG[u], bacc[u], self.b_pv], [bacc[u]], lambda e, kk=kk: e.scalar_tensor_tensor(
                            out=acc[:, u, :], in0=G[:, u, kk:TT + kk], scalar=self.pv[:, gco + kk * 12 + c:gco + kk * 12 + c + 1],
                            in1=acc[:, u, :], op0=ALU.mult, op1=ALU.add))
                def partA2(c, u):
                    if c >= 8:
                        kb.op("act", [bacc[u]], [bQKV[s]], lambda e: e.activation(
                            out=QKV[:, 0, c, :], in_=acc[:, u, :], func=AF.Silu))
                    else:
                        kb.op("act", [bacc[u]], [bsg[u]], lambda e: e.activation(
                            out=sg[:, u, :], in_=acc[:, u, :], func=AF.Silu))
                        kb.op("act", [bsg[u]], [bsq[u]], lambda e: e.activation(
                            out=sq[:, u, :], in_=sg[:, u, :], func=AF.Square))

                def partB(c, u):
                    um = u % 2
                    pm, bpm = self.ps[4 + um], self.b_ps[4 + um]
                    kb.op("pe", [bsq[u], bonesb], [bpm], lambda e: e.matmul(
                        pm[:], lhsT=onesb[:], rhs=sq[:, u, :], start=True, stop=True))
                    kb.op("act", [bpm], [bsq[u]], lambda e: e.activation(
                        out=sq[:, u, :], in_=pm[:], func=AF.Ln, bias=self.C("eps"), scale=1.0))
                    kb.op("act", [bsq[u]], [brs[u]], lambda e: e.activation(
                        out=rs[:, u, :], in_=sq[:, u, :], func=AF.Exp, scale=-0.5))
                    sc_ = (128.0 ** -0.5) if c < 4 else 1.0
                    kb.op("dve", [bsg[u], brs[u]], [bQKV[s]], lambda e: e.scalar_tensor_tensor(
                        out=QKV[:, 0, c, :], in0=sg[:, u, :], scalar=sc_, in1=rs[:, u, :],
                        op0=ALU.mult, op1=ALU.mult))

                seq = [("A", 0), ("A", 1), ("A2", 0), ("A", 2), ("A2", 1), ("A", 3), ("A2", 2), ("A2", 3),
                       ("B", 0), ("B", 1), ("A", 4), ("A2", 4), ("B", 2), ("A", 5), ("A2", 5), ("B", 3),
                       ("A", 6), ("A2", 6), ("A", 7), ("A2", 7), ("A", 8), ("B", 4), ("A", 9), ("A2", 8), ("B", 5),
                       ("A", 10), ("A2", 9), ("B", 6), ("A", 11), ("A2", 10), ("B", 7), ("A2", 11)]
                for kind_, c in seq:
                    u = c % 4
                    if kind_ == "A":
                        partA(c, u)
                    elif kind_ == "A2":
                        partA2(c, u)
                    else:
                        partB(c, u)
                n += 12
                kb.dma("pool", self.qkvT[:, :, j * TT:(j + 1) * TT].rearrange("k p t -> p k t"), QKV[:, 0],
                       reads=[bQKV[s]])
                for a in range(4):
                    for hf in range(2):
                        u = 6 + n % 2
                        n += 1
                        pp, bpp = self.ps[u], self.b_ps[u]
                        for q_ in range(4):
                            c = 4 + hf * 4 + q_
                            kb.op("pe", [bQKV[s], self.b_cst], [bpp], lambda e, a=a, c=c, q_=q_: e.transpose(
                                pp[:, q_ * 128:(q_ + 1) * 128], QKV[:, 0, c, a * 128:(a + 1) * 128], self.C("ident")))
                        kb.op("act", [bpp], [bKVt[s]], lambda e, a=a, hf=hf: e.copy(
                            out=KVt[:, 0, a, hf * 512:(hf + 1) * 512], in_=pp[:]))
                kb.dma("pool", self.kvtok[j * TT:(j + 1) * TT, :].rearrange("(a p) c -> p a c", p=128), KVt[:, 0],
                       reads=[bKVt[s]])
                for a in range(4):
                    u = 6 + n % 2
                    n += 1
                    pp, bpp = self.ps[u], self.b_ps[u]
                    for k in range(KC):
                        kb.op("pe", [bW, bh[s]], [bpp], lambda e, k=k, a=a: e.matmul(
                            pp[:], lhsT=h[:, s, k, a * 128:(a + 1) * 128], rhs=W[:, k, 1536:2048],
                            start=(k == 0), stop=(k == KC - 1)))
                    kb.op("act", [bpp], [bZt[s]], lambda e, a=a: e.activation(out=Zt[:, 0, a, :], in_=pp[:], func=AF.Silu))
                for a in range(4):
                    u = 6 + n % 2
                    n += 1
                    pp, bpp = self.ps[u], self.b_ps[u]
                    for k in range(KC):
                        kb.op("pe", [bW, bh[s]], [bpp], lambda e, k=k, a=a: e.matmul(
                            pp[:, 0:8], lhsT=h[:, s, k, a * 128:(a + 1) * 128], rhs=W[:, k, 2048:2056],
                            start=(k == 0), stop=(k == KC - 1)))
                    kb.op("act", [bpp], [bbt[a]], lambda e, a=a: e.copy(out=bt[:, a, 0:8], in_=pp[:, 0:8]))
                for a in range(4):
                    kb.op("act", [bbt[a]], [bBG[s]], lambda e, a=a: e.activation(
                        out=BG[:, s, a, 0:4], in_=bt[:, a, 0:4], func=AF.Sigmoid))
                for a in range(4):
                    kb.op("dve", [bbt[a], self.b_pv], [bbt[a]], lambda e, a=a: e.tensor_tensor(
                        out=bt[:, a, 0:4], in0=bt[:, a, 4:8], in1=self.P(f"dtb{li}"), op=ALU.add))
                for a in range(4):
                    kb.op("act", [bbt[a]], [bbt[a]], lambda e, a=a: e.activation(
                        out=bt[:, a, 4:8], in_=bt[:, a, 0:4], func=AF.Exp))
                for a in range(4):
                    kb.op("act", [bbt[a]], [bbt[a]], lambda e, a=a: e.activation(
                        out=bt[:, a, 0:4], in_=bt[:, a, 4:8], func=AF.Ln, bias=self.C("one"), scale=1.0))
                for a in range(4):
                    kb.op("dve", [bbt[a], bea], [bBG[s]], lambda e, a=a: e.scalar_tensor_tensor(
                        out=BG[:, s, a, 4:8], in0=bt[:, a, 0:4], scalar=-1.0, in1=ea[:], op0=ALU.mult, op1=ALU.mult))
                kb.dma("pool", self.ztok[j * TT:(j + 1) * TT, :].rearrange("(a p) c -> p a c", p=128), Zt[:, 0],
                       reads=[bZt[s]])
                kb.dma("pool", self.bgtok[j * TT:(j + 1) * TT, :].rearrange("(a p) c -> p a c", p=128), BG[:, s],
                       reads=[bBG[s]])
                def pool1(g, win):
                    u = g % 2
                    pp, bpp = self.ps[u], self.b_ps[u]
                    pm, bpm = self.ps[4 + u], self.b_ps[4 + u]
                    for k in range(KC):
                        kb.op("pe", [bW, bh[s]], [bpp], lambda e, k=k, g=g: e.matmul(
                            pp[:], lhsT=W[:, k, 2056 + g * 128:2056 + (g + 1) * 128], rhs=h[:, s, k, :],
                            start=(k == 0), stop=(k == KC - 1)))
                    kb.op("act", [bHp], [bPB[u]], lambda e, g=g, u=u: e.copy(out=PB[:, u, 0:15], in_=Hp[:, g, :]))
                    kb.op("act", [bpp], [bPB[u]], lambda e, u=u: e.copy(out=PB[:, u, 15:TT + 15], in_=pp[:]))
                    kb.op("act", [bpp], [bHp], lambda e, g=g: e.copy(out=Hp[:, g, :], in_=pp[:, TT - 15:TT]))
                    src, bsrc = PB, bPB[u]
                    sh = 1
                    bufs = [(S2, bS2[u]), (S4, bS4[u])]
                    lvl = 0
                    while sh < win:
                        dst, bdst = bufs[lvl % 2]
                        eng = "dve"
                        kb.op(eng, [bsrc], [bdst], lambda e, src=src, dst=dst, sh=sh, u=u: e.tensor_tensor(
                            out=dst[:, u, sh:TT + 15], in0=src[:, u, sh:TT + 15], in1=src[:, u, 0:TT + 15 - sh], op=ALU.add))
                        src, bsrc = dst, bdst
                        sh *= 2
                        lvl += 1
                    if j == 0:
                        kb.op("dve", [bsrc, bc2], [bacc[u]], lambda e, src=src, u=u, g=g: e.tensor_tensor(
                            out=acc[:, u, :], in0=src[:, u, 15:TT + 15], in1=C2(f"rc{g}"), op=ALU.mult))
                        kb.op("dve", [bacc[u], bPB[u]], [bpl[u]], lambda e, u=u: e.tensor_tensor(
                            out=pl[:, u, :], in0=acc[:, u, :], in1=PB[:, u, 15:TT + 15], op=ALU.subtract))
                    else:
                        kb.op("dve", [bsrc, bPB[u]], [bpl[u]], lambda e, src=src, u=u, win=win: e.scalar_tensor_tensor(
                            out=pl[:, u, :], in0=src[:, u, 15:TT + 15], scalar=1.0 / win, in1=PB[:, u, 15:TT + 15],
                            op0=ALU.mult, op1=ALU.subtract))
                def pool2(g):
                    u = g % 2
                    pm, bpm = self.ps[4 + u], self.b_ps[4 + u]
                    kb.op("pe", [bPW, bpl[u]], [bpm], lambda e, g=g, u=u: e.matmul(
                        pm[:], lhsT=PW[:, g, :], rhs=pl[:, u, :], start=True, stop=True))
                    kb.op("act", [bpm, self.b_pv], [bOB[s]], lambda e, g=g: e.activation(
                        out=OB[:, s, g, :], in_=pm[:], func=AF.Copy, scale=self.pv[:, pso + g:pso + g + 1]))
                wins = (2, 4, 8, 16)
                pool1(0, wins[0])
                pool1(1, wins[1])
                pool2(0)
                pool1(2, wins[2])
                pool2(1)
                pool1(3, wins[3])
                pool2(2)
                pool2(3)
                kb.dma("pool", self.oab[4:8, :, j * TT:(j + 1) * TT].rearrange("k p t -> p k t"), OB[:, s],
                       reads=[bOB[s]])
            kb.barrier()

    def phase_gdn(self, li):
        nc, kb = self.nc, self.kb
        T = self.T
        NG = 2
        NSC = T // 128
        ngroups = NSC // NG
        with ExitStack() as st:
            KQ = self.sb(st, "KQ", [128, 2, 8, NG * 128], F32)
            bKQ = kb.bufs(2, "KQ")
            KVT = self.sb(st, "KVT", [128, 2, NG, D], F32)
            bKVT = kb.bufs(2, "KVT")
            BGt = self.sb(st, "BGt", [128, 2, NG, 8], F32)
            bBGt = kb.bufs(2, "BGt")
            ZT = self.sb(st, "ZT", [128, 2, NG, 512], F32)
            bZT = kb.bufs(2, "ZT")
            OA = self.sb(st, "OA", [128, 2, 4, NG * 128], BF16)
            bOA = kb.bufs(2, "OA")
            NCH = 4 * NG
            NTL = 26
            WK = self.sb(st, "WK", [128, NCH, NTL, 128], F32)
            Sst = self.sb(st, "Sst", [128, 4, 128], F32)
            bS = kb.bufs(4, "S")
            vn = self.sb(st, "vn", [128, 4, 128], F32)
            bvn = kb.bufs(4, "vn")
            sm = self.sb(st, "sm", [128, NCH, 8], F32)
            kb.op("pool", [], bS, lambda e: e.memset(Sst[:], 0.0))
            kb.op("pool", [], bvn, lambda e: e.memset(vn[:], 0.0))
            kb.op("pool", [], [], lambda e: e.memset(WK[:], 0.0))
            kb.barrier()
            bq = [[kb.buf("q")] * 4 for _ in range(8)]
            gno, _ = self.pvoff[f"gnrow{li}"]
            names = ["Gt", "dec", "decT", "egr", "Y0", "Y1", "X0", "X1", "R0", "R1", "intraT", "ub", "kbg",
                     "kd0", "kd1", "qg", "u", "wT", "o", "t1", "t2", "junk"]
            TI = {nm: i for i, nm in enumerate(names)}

            class Ch:
                pass

            allq = set(id(x[0]) for x in bq)
            kb_op = kb.op

            def xop(eng, reads, writes, fn):
                if eng != "pe":
                    extra = [r for r in reads if id(r) in allq and r not in writes]
                    writes = list(writes) + extra
                kb_op(eng, reads, writes, fn)

            chain_b = [{nm: kb.buf(nm) for nm in names} for _ in range(NCH)]
            chain_bsm = kb.bufs(NCH, "sm")

            def load(gi):
                s = gi % 2
                t0 = gi * NG * 128
                t1 = t0 + NG * 128
                kb.dma("sp", KQ[:, s], self.qkvT[0:8, :, t0:t1].rearrange("k p t -> p k t"), writes=[bKQ[s]])
                kb.dma("sp", KVT[:, s], self.kvtok[t0:t1, :].rearrange("(m p) c -> p m c", p=128), writes=[bKVT[s]])
                kb.dma("sp", BGt[:, s], self.bgtok[t0:t1, :].rearrange("(m p) c -> p m c", p=128), writes=[bBGt[s]])
                kb.dma("sp", ZT[:, s], self.ztok[t0:t1, :].rearrange("(m p) c -> p m c", p=128), writes=[bZT[s]])

            ident, ones = self.C("ident"), self.C("ones")
            C2, bc2 = self.load_c2(st, "Ubd", "MnIT")
            Ubd, nUbd, SLbd, MnS, MnIT = (C2(x) for x in ("Ubd", "nUbd", "SLbd", "MnS", "MnIT"))
            bc = self.b_cst
            kb.barrier()
            load(0)
            for gi in range(ngroups):
                s = gi % 2
                if gi + 1 < ngroups:
                    load(gi + 1)
                chains = []
                for m in range(NG):
                    for hh in range(4):
                        c = Ch()
                        c.i = m * 4 + hh
                        c.m, c.hh = m, hh
                        c.bank = c.i
                        c.Q = [self.ps[c.bank][:, qq * 128:(qq + 1) * 128] for qq in range(4)]
                        c.bQ = bq[c.bank]
                        c.t = {nm: WK[:, c.i, TI[nm], :] for nm in names}
                        c.b = chain_b[c.i]
                        c.sm = sm[:, c.i, :]
                        c.bsm = chain_bsm[c.i]
                        c.kT = KQ[:, s, 4 + hh, m * 128:(m + 1) * 128]
                        c.qT = KQ[:, s, hh, m * 128:(m + 1) * 128]
                        c.ktok = KVT[:, s, m, hh * 128:(hh + 1) * 128]
                        c.vtok = KVT[:, s, m, 512 + hh * 128:512 + (hh + 1) * 128]
                        c.beta = BGt[:, s, m, hh:hh + 1]
                        c.g = BGt[:, s, m, 4 + hh:5 + hh]
                        c.z = ZT[:, s, m, hh * 128:(hh + 1) * 128]
                        chains.append(c)
                for c in chains:
                    xop("dve", [bBGt[s], bc], [c.b["Gt"]], lambda e, c=c: e.tensor_scalar(
                        out=c.t["Gt"], in0=ones, scalar1=c.g, scalar2=None, op0=ALU.mult))
                    xop("pool", [bBGt[s]], [c.bsm], lambda e, c=c: e.tensor_scalar(
                        out=c.sm[:, 2:3], in0=c.beta, scalar1=-1.0, scalar2=None, op0=ALU.mult))
                for c in chains:
                    Gt = c.t["Gt"]
                    rd = [c.b["Gt"], bc]
                    xop("pe", rd, [c.bQ[2]], lambda e: e.matmul(c.Q[2], lhsT=Gt, rhs=Ubd, start=True, stop=True))
                    xop("pe", [bBGt[s], bc], [c.bQ[3]], lambda e: e.matmul(c.Q[3][:, 0:1], lhsT=Ubd, rhs=c.g, start=True, stop=True))
                    xop("pe", [bBGt[s], bc], [c.bQ[3]], lambda e: e.matmul(c.Q[3][:, 1:2], lhsT=SLbd, rhs=c.g, start=True, stop=True))
                for c in chains:
                    xop("act", [c.bQ[3]], [c.bsm], lambda e: e.copy(out=c.sm[:, 6:7], in_=c.Q[3][:, 0:1]))
                    xop("act", [c.bQ[3]], [c.bsm], lambda e: e.activation(out=c.sm[:, 7:8], in_=c.Q[3][:, 0:1], func=AF.Copy, scale=-1.0))
                    xop("act", [c.bQ[3]], [c.bsm], lambda e: e.activation(out=c.sm[:, 0:2], in_=c.Q[3][:, 0:2], func=AF.Exp))
                    xop("act", [c.bQ[2]], [c.b["egr"]], lambda e: e.activation(out=c.t["egr"], in_=c.Q[2], func=AF.Exp))
                    xop("dve", [c.bQ[2], bc], [c.b["dec"]], lambda e: e.tensor_tensor(
                        out=c.t["dec"], in0=MnS, in1=c.Q[2], op=ALU.subtract))
                    xop("dve", [c.bQ[2], bc], [c.b["decT"]], lambda e: e.tensor_tensor(
                        out=c.t["decT"], in0=c.Q[2], in1=MnIT, op=ALU.add))
                for c in chains:
                    xop("act", [c.b["dec"], c.bsm], [c.b["dec"]], lambda e: e.activation(
                        out=c.t["dec"], in_=c.t["dec"], func=AF.Exp, bias=c.sm[:, 6:7], scale=1.0))
                    xop("act", [c.b["decT"], c.bsm], [c.b["decT"]], lambda e: e.activation(
                        out=c.t["decT"], in_=c.t["decT"], func=AF.Exp, bias=c.sm[:, 7:8], scale=1.0))
                for c in chains:
                    xop("pe", [bKQ[s]], [c.bQ[0]], lambda e: e.matmul(c.Q[0], lhsT=c.kT, rhs=c.kT, start=True, stop=True))
                    xop("pe", [bKQ[s]], [c.bQ[1]], lambda e: e.matmul(c.Q[1], lhsT=c.kT, rhs=c.qT, start=True, stop=True))
                for c in chains:
                    xop("dve", [c.bQ[0], c.bsm, c.b["dec"]], [c.b["Y0"]], lambda e: e.scalar_tensor_tensor(
                        out=c.t["Y0"], in0=c.Q[0], scalar=c.sm[:, 2:3], in1=c.t["dec"], op0=ALU.mult, op1=ALU.mult))
                    xop("dve", [c.bQ[1], c.b["decT"]], [c.b["intraT"]], lambda e: e.tensor_tensor(
                        out=c.t["intraT"], in0=c.Q[1], in1=c.t["decT"], op=ALU.mult))
                    xop("act", [bKVT[s], bBGt[s]], [c.b["ub"]], lambda e: e.activation(
                        out=c.t["ub"], in_=c.vtok, func=AF.Copy, scale=c.beta))
                    xop("pool", [bKVT[s], bBGt[s], c.bsm], [c.b["kbg"]], lambda e: e.tensor_scalar(
                        out=c.t["kbg"], in0=c.ktok, scalar1=c.beta, scalar2=c.sm[:, 0:1], op0=ALU.mult, op1=ALU.mult))
                    xop("act", [bKVT[s], c.bsm], [c.b["kd0"]], lambda e: e.activation(
                        out=c.t["kd0"][0:64, :], in_=c.ktok[0:64, :], func=AF.Copy, scale=c.sm[0:64, 1:2]))
                    xop("act", [bKVT[s], c.bsm], [c.b["kd1"]], lambda e: e.activation(
                        out=c.t["kd1"][64:128, :], in_=c.ktok[64:128, :], func=AF.Copy, scale=c.sm[64:128, 1:2]))
                    xop("pool", [bKQ[s], c.b["egr"]], [c.b["qg"]], lambda e: e.tensor_tensor(
                        out=c.t["qg"], in0=c.qT, in1=c.t["egr"], op=ALU.mult))
                for c in chains:
                    xop("pe", [c.b["Y0"], bc], [c.bQ[2]], lambda e: e.transpose(c.Q[2], c.t["Y0"], ident))
                for c in chains:
                    xop("act", [c.bQ[2]], [c.b["X0"]], lambda e: e.copy(out=c.t["X0"], in_=c.Q[2]))
                    xop("dve", [c.bQ[2], bc], [c.b["R0"]], lambda e: e.tensor_tensor(
                        out=c.t["R0"], in0=c.Q[2], in1=ident, op=ALU.add))
                cur = 0
                for lvl in range(1, 6):
                    nx = 1 - cur
                    X, Y, R = f"X{cur}", f"Y{cur}", f"R{cur}"
                    Xn, Yn, Rn = f"X{nx}", f"Y{nx}", f"R{nx}"
                    for c in chains:
                        if lvl < 5:
                            xop("pe", [c.b[X], c.b[Y]], [c.bQ[0]], lambda e: e.matmul(
                                c.Q[0], lhsT=c.t[Y], rhs=c.t[X], start=True, stop=True))
                        xop("pe", [c.b[X], c.b[Y]], [c.bQ[1]], lambda e: e.matmul(
                            c.Q[1], lhsT=c.t[X], rhs=c.t[Y], start=True, stop=True))
                    for c in chains:
                        if lvl < 5:
                            xop("act", [c.bQ[0]], [c.b[Xn]], lambda e: e.copy(out=c.t[Xn], in_=c.Q[0]))
                        xop("dve", [c.bQ[1]], [c.b[Yn]], lambda e: e.tensor_copy(out=c.t[Yn], in_=c.Q[1]))
                    for c in chains:
                        xop("pe", [c.b[R], c.b[Yn]], [c.bQ[2]], lambda e: e.matmul(
                            c.Q[2], lhsT=c.t[Yn], rhs=c.t[R], start=True, stop=True))
                    for ci, c in enumerate(chains):
                        xop("dve", [c.bQ[2], c.b[R]], [c.b[Rn]], lambda e: e.tensor_tensor(
                            out=c.t[Rn], in0=c.Q[2], in1=c.t[R], op=ALU.add))
                    cur = nx
                Rf = f"R{cur}"
                for c in chains:
                    xop("pe", [c.b[Rf], c.b["ub"]], [c.bQ[0]], lambda e: e.matmul(
                        c.Q[0], lhsT=c.t[Rf], rhs=c.t["ub"], start=True, stop=True))
                    xop("pe", [c.b[Rf], c.b["kbg"]], [c.bQ[1]], lambda e: e.matmul(
                        c.Q[1], lhsT=c.t["kbg"], rhs=c.t[Rf], start=True, stop=True))
                for c in chains:
                    xop("act", [c.bQ[0]], [c.b["u"]], lambda e: e.copy(out=c.t["u"], in_=c.Q[0]))
                    xop("dve", [c.bQ[1]], [c.b["wT"]], lambda e: e.tensor_copy(out=c.t["wT"], in_=c.Q[1]))
                for m in range(NG):
                    cs_ = [c for c in chains if c.m == m]
                    for hf in range(2):
                        rng = slice(64 * hf, 64 * hf + 64)
                        for c in cs_:
                            hh = c.hh
                            xop("pe", [c.b["wT"], bS[hh]], [c.bQ[3]], lambda e: e.matmul(
                                c.Q[3], lhsT=c.t["wT"], rhs=Sst[:, hh, :], start=True, stop=True))
                        for c in cs_:
                            hh = c.hh
                            xop("dve", [c.b["u"], c.bQ[3]], [bvn[hh]], lambda e: e.tensor_tensor(
                                out=vn[rng, hh, :], in0=c.t["u"][rng, :], in1=c.Q[3][rng, :], op=ALU.subtract))
                        for c in cs_:
                            hh = c.hh
                            xop("pe", [c.b["qg"], bS[hh]], [c.bQ[0]], lambda e: e.matmul(
                                c.Q[0], lhsT=c.t["qg"], rhs=Sst[:, hh, :], start=True, stop=False))
                            xop("pe", [c.b["intraT"], bvn[hh]], [c.bQ[0]], lambda e: e.matmul(
                                c.Q[0], lhsT=c.t["intraT"], rhs=vn[:, hh, :], start=False, stop=True))
                            kd = "kd0" if hf == 0 else "kd1"
                            xop("pe", [c.b[kd], bvn[hh]], [c.bQ[1]], lambda e: e.matmul(
                                c.Q[1], lhsT=c.t[kd], rhs=vn[:, hh, :], start=True, stop=True))
                        for c in cs_:
                            hh = c.hh
                            gl = c.t["egr"][:, 64 * hf + 63:64 * hf + 64]
                            xop("dve", [bS[hh], c.b["egr"], c.bQ[1]], [bS[hh]], lambda e: e.scalar_tensor_tensor(
                                out=Sst[:, hh, :], in0=Sst[:, hh, :], scalar=gl, in1=c.Q[1], op0=ALU.mult, op1=ALU.add))
                            xop("act", [c.bQ[0]], [c.b["o"]], lambda e: e.copy(out=c.t["o"][rng, :], in_=c.Q[0][rng, :]))
                for c in chains:
                    xop("act", [c.b["o"]], [c.b["junk"]], lambda e: e.activation(
                        out=c.t["junk"], in_=c.t["o"], func=AF.Square))
                    xop("dve", [c.b["junk"]], [c.bsm], lambda e: e.reduce_sum(
                        out=c.sm[:, 3:4], in_=c.t["junk"], axis=mybir.AxisListType.X))
                    xop("act", [c.bsm], [c.bsm], lambda e: e.activation(
                        out=c.sm[:, 4:5], in_=c.sm[:, 3:4], func=AF.Sqrt, bias=self.C("eps"), scale=1.0 / 128))
                    xop("dve", [c.bsm], [c.bsm], lambda e: e.reciprocal(out=c.sm[:, 5:6], in_=c.sm[:, 4:5]))
                    xop("dve", [c.b["o"], c.bsm, self.b_pv], [c.b["t1"]], lambda e: e.scalar_tensor_tensor(
                        out=c.t["t1"], in0=c.t["o"], scalar=c.sm[:, 5:6], in1=self.pv[:, gno:gno + 128],
                        op0=ALU.mult, op1=ALU.mult))
                    xop("pool", [c.b["t1"], bZT[s]], [c.b["t2"]], lambda e: e.tensor_tensor(
                        out=c.t["t2"], in0=c.t["t1"], in1=c.z, op=ALU.mult))
                for c in chains:
                    xop("pe", [c.b["t2"], bc], [c.bQ[2]], lambda e: e.transpose(c.Q[2], c.t["t2"], ident))
                for c in chains:
                    xop("act", [c.bQ[2]], [bOA[s]], lambda e: e.copy(
                        out=OA[:, s, c.hh, c.m * 128:(c.m + 1) * 128], in_=c.Q[2]))
                t0 = gi * NG * 128
                kb.dma("pool", self.oab[0:4, :, t0:t0 + NG * 128].rearrange("k p t -> p k t"), OA[:, s], reads=[bOA[s]])
            kb.barrier()

    def build(self, plan):
        kb = self.kb
        with ExitStack() as st:
            self.setup_globals(st)
            self.phase_mod()
            self.phase_in()
            for step in plan:
                kind, li = step
                if kind in ("evenproj", "even"):
                    self.phase_norm(li, "m", self.xs)
                    self.phase_even_proj(li)
                if kind == "even":
                    self.phase_gdn(li)
                    self.phase_proj_res("eo", self.oab, KC, self.ev_w_out[li // 2], li, "g_m", self.xs, self.xs)
                if kind == "odd":
                    self.phase_norm(li, "m", self.xs)
                    for g in range(3):
                        self.phase_odd_proj(li, g)
                    self.phase_attn(li)
                    self.phase_proj_res("ao", self.attn, KC, self.od_w_out[li // 2], li, "g_m", self.xs, self.xs)
                if kind == "ffn":
                    self.phase_ffn_up(li, self.xs)
                    self.phase_proj_res("dn", self.act, FC, self.ffn_w_down[li], li, "g_f", self.xs, self.xs)
            self.phase_out()
        kb.close()
        return self.nc


def pack_alibi():
    j = np.arange(128, dtype=np.float64)[:, None]
    a = np.arange(128, dtype=np.float64)[None, :]
    out = np.zeros((128, 3, 8, 4, 128), dtype=ml_dtypes.bfloat16)
    NEG = -30000.0
    for g, dil in enumerate((1, 4, 16)):
        for h in range(8):
            s = (2.0 ** (-(h + 1))) * dil
            prev = np.where(j >= a, -s * (128 + a - j), NEG)
            cur = np.where(j <= a, -s * (a - j), NEG)
            for idx, m in ((0, prev), (2, cur)):
                hi = m.astype(np.float32).astype(ml_dtypes.bfloat16)
                lo = (m - hi.astype(np.float64)).astype(np.float32).astype(ml_dtypes.bfloat16)
                out[:, g, h, idx, :] = hi
                out[:, g, h, idx + 1, :] = lo
    return np.ascontiguousarray(out.reshape(128, -1))


_T_FULL = 8192
_DEPTH = 4
_NCORES = 8
_ACTIVE = [0, 1, 4, 5]


def full_plan(depth):
    plan = []
    for i in range(depth):
        plan.append(("even" if i % 2 == 0 else "odd", i))
        plan.append(("ffn", i))
    return plan


def make_in_maps(inp, depth, ncores):
    B = inp["x"].shape[0]
    cst = pack_cst()
    alibi = pack_alibi()
    f32 = lambda a: np.ascontiguousarray(np.asarray(a, np.float32))
    shared = {
        "cst": cst, "cst2": pack_cst2(), "alibi": alibi,
        "ada_w": f32(inp["ada_w"]), "ffn_w_up": f32(inp["ffn_w_up"]), "ffn_w_down": f32(inp["ffn_w_down"]),
        "ev_w_in": f32(inp["ev_w_in"]), "ev_w_out": f32(inp["ev_w_out"]), "pool_w": f32(inp["pool_w"]),
        "od_w_in": f32(inp["od_w_in"]), "od_w_out": f32(inp["od_w_out"]),
    }
    maps = []
    zx = None
    for c in range(ncores):
        m = dict(shared)
        if c in _ACTIVE:
            b = _ACTIVE.index(c)
            m["x"] = f32(inp["x"][b])
            m["pv"] = pack_pv(inp, b, depth)
        else:
            if zx is None:
                zx = np.zeros_like(f32(inp["x"][0]))
            m["x"] = zx
            m["pv"] = pack_pv(inp, 0, depth)
        maps.append(m)
    return maps


def kernel(**inputs):
    inp = {k: np.asarray(v) for k, v in inputs.items()}
    B, T, _ = inp["x"].shape
    prog = Prog(T, _DEPTH)
    nc = prog.build(full_plan(_DEPTH))
    maps = make_in_maps(inp, _DEPTH, _NCORES)
    res = run_bass_kernel_spmd(nc, maps, core_ids=list(range(_NCORES)))
    out = np.stack([np.asarray(res.results[_ACTIVE[b]]["y"], np.float32) for b in range(B)], axis=0)
    return out
```

```python
import numpy as np
import ml_dtypes
from contextlib import ExitStack
import concourse.bass as bass
import concourse.mybir as mybir
from concourse.bass_utils import run_bass_kernel_spmd

F32 = mybir.dt.float32
BF16 = mybir.dt.bfloat16
AF = mybir.ActivationFunctionType
ALU = mybir.AluOpType

D = 1024
KC = 8
DFF = 2816
FC = 22
EPS = 1e-6
TT = 512


class Buf:
    __slots__ = ("name", "w", "r", "dch", "sch")

    def __init__(self, name):
        self.name = name
        self.w = None
        self.r = {}
        self.dch = None
        self.sch = None


class Chan:
    __slots__ = ("sem", "val", "kind")

    def __init__(self, sem, val):
        self.sem = sem
        self.val = val
        self.kind = None


class Eng:
    def __init__(self, name, handle, sem):
        self.name = name
        self.h = handle
        self.sem = sem
        self.cnt = 0
        self.waited = {}


class KB:
    def __init__(self, nc):
        self.nc = nc
        self.gstack = ExitStack()
        self.E = {}
        for name, h in (("pe", nc.tensor), ("act", nc.scalar), ("dve", nc.vector),
                        ("pool", nc.gpsimd), ("sp", nc.sync)):
            self.E[name] = Eng(name, h, self._newsem(name))
        self.free_ch = {}
        self.live_ch = []
        self.nsem = 5
        self.uid = 0

    def _newsem(self, name):
        self.semid = getattr(self, "semid", 0) + 1
        return self.gstack.enter_context(self.nc.semaphore(f"s_{name}_{self.semid}"))

    def chan(self, kind):
        fl = self.free_ch.setdefault(kind, [])
        while fl:
            c = fl.pop()
            if c.val < 24000:
                self.live_ch.append(c)
                return c
        self.nsem += 1
        c = Chan(self._newsem("ch" + kind), 0)
        c.kind = kind
        self.live_ch.append(c)
        return c

    def buf(self, name="b"):
        self.uid += 1
        return Buf(f"{name}{self.uid}")

    def bufs(self, n, name="b"):
        return [self.buf(name) for _ in range(n)]

    def _wait(self, e, dep):
        sem, val, src = dep
        if src == e.name and e.name in ("pe", "sp"):
            return
        k = id(sem)
        if e.waited.get(k, 0) >= val:
            return
        e.h.wait_ge(sem, val)
        e.waited[k] = val

    def _deps(self, e, reads, writes, dma_ch=None):
        for b in reads:
            if b.w is not None:
                self._wait(e, b.w)
        for b in writes:
            if b.w is not None:
                if not (dma_ch is not None and b.w[0] is dma_ch.sem):
                    self._wait(e, b.w)
            for dep in b.r.values():
                self._wait(e, dep)

    def op(self, eng, reads, writes, fn):
        e = self.E[eng]
        self._deps(e, reads, writes)
        ins = fn(e.h)
        ins.then_inc(e.sem, 1)
        e.cnt += 1
        tag = (e.sem, e.cnt, e.name)
        for b in writes:
            b.w = tag
            b.r = {}
        for b in reads:
            b.r[e.name] = tag

    def dma(self, q, out, in_, reads=(), writes=(), **kw):
        e = self.E[q]
        kind = "sw" if q == "pool" else "hw"
        if writes:
            b0 = writes[0]
            if b0.dch is None:
                b0.dch = self.chan(kind)
            ch = b0.dch
        else:
            b0 = reads[0]
            if b0.sch is None:
                b0.sch = self.chan(kind)
            ch = b0.sch
        assert ch.kind == kind
        for b in reads:
            if b.w is not None:
                self._wait(e, b.w)
        for b in writes:
            if b.w is not None and b.w[0] is not ch.sem:
                self._wait(e, b.w)
            for dep in b.r.values():
                self._wait(e, dep)
        e.h.dma_start(out=out, in_=in_, **kw).then_inc(ch.sem, 16)
        ch.val += 16
        tag = (ch.sem, ch.val, "dma")
        for b in writes:
            b.w = tag
            b.r = {}
        for b in reads:
            b.r[id(ch)] = tag

    def barrier(self):
        evs = []
        for e in self.E.values():
            if e.name != "sp" and e.cnt > 0:
                evs.append((e.sem, e.cnt, e.name))
        for c in self.live_ch:
            if c.val > 0:
                evs.append((c.sem, c.val, "dma"))
        for e in self.E.values():
            for ev in evs:
                if ev[2] == e.name:
                    continue
                self._wait(e, ev)
        for c in self.live_ch:
            self.free_ch.setdefault(c.kind, []).append(c)
        self.live_ch = []
        for e in self.E.values():
            if e.cnt > 8000:
                self.nsem += 1
                e.sem = self._newsem(e.name)
                e.cnt = 0

    def close(self):
        self.gstack.close()


def _pcol(v, n=None):
    v = np.asarray(v, np.float32).reshape(-1, 128)
    return np.ascontiguousarray(v.T)


class Packer:
    def __init__(self):
        self.cols = []
        self.off = {}
        self.n = 0

    def add(self, name, arr):
        arr = np.asarray(arr, np.float32)
        assert arr.shape[0] == 128
        arr = arr.reshape(128, -1)
        self.off[name] = (self.n, arr.shape[1])
        self.n += arr.shape[1]
        self.cols.append(arr)

    def pack(self):
        return np.ascontiguousarray(np.concatenate(self.cols, axis=1))


def pv_layout(depth):
    L = []
    L.append(("c", 8))
    for i in range(depth):
        L += [(f"nm{i}", 8), (f"nf{i}", 8), (f"adab{i}", 48), (f"fcw{i}", 66), (f"fcb{i}", 22)]
        if i % 2 == 0:
            L += [(f"gcw{i}", 48), (f"gnorm{i}", 1), (f"pscale{i}", 4), (f"alog{i}", 4), (f"dtb{i}", 4), (f"gnrow{i}", 128)]
        else:
            L += [(f"qn{i}", 1), (f"kn{i}", 1)]
    off = {}
    n = 0
    for k, w in L:
        off[k] = (n, w)
        n += w
    return off, n


def pack_pv(inp, b, depth):
    P = Packer()
    P.add("c", _pcol(inp["c"][b]))
    for i in range(depth):
        P.add(f"nm{i}", _pcol(inp["norm_mix"][i]))
        P.add(f"nf{i}", _pcol(inp["norm_ffn"][i]))
        P.add(f"adab{i}", _pcol(inp["ada_b"][i]))
        cw = inp["ffn_conv_w"][i]
        P.add(f"fcw{i}", np.concatenate([_pcol(cw[k]) for k in range(3)], axis=1))
        P.add(f"fcb{i}", _pcol(inp["ffn_conv_b"][i]))
        if i % 2 == 0:
            e = i // 2
            gw = inp["gdn_conv_w"][e]
            P.add(f"gcw{i}", np.concatenate([_pcol(gw[k]) for k in range(4)], axis=1))
            P.add(f"gnorm{i}", _pcol(inp["gdn_norm"][e]))
            P.add(f"pscale{i}", _pcol(inp["pool_scale"][e]))
            P.add(f"alog{i}", np.tile(np.asarray(inp["gdn_a_log"][e], np.float32)[None, :], (128, 1)))
            P.add(f"dtb{i}", np.tile(np.asarray(inp["gdn_dt_bias"][e], np.float32)[None, :], (128, 1)))
            P.add(f"gnrow{i}", np.tile(np.asarray(inp["gdn_norm"][e], np.float32)[None, :], (128, 1)))
        else:
            o = i // 2
            P.add(f"qn{i}", _pcol(inp["att_q_norm"][o]))
            P.add(f"kn{i}", _pcol(inp["att_k_norm"][o]))
    off, n = pv_layout(depth)
    assert P.off == off and P.n == n
    return P.pack()


def cst_layout(which=1):
    L = [("ident", 128), ("ones", 128), ("eps", 1), ("one", 1)]
    if which == 2:
        L = [("rc0", TT), ("rc1", TT), ("rc2", TT), ("rc3", TT),
             ("Ubd", 128), ("nUbd", 128), ("SLbd", 128), ("MnS", 128), ("MnIT", 128)]
    off = {}
    n = 0
    for k, w in L:
        off[k] = (n, w)
        n += w
    return off, n


def pack_cst():
    P = Packer()
    P.add("ident", np.eye(128, dtype=np.float32))
    P.add("ones", np.ones((128, 128), np.float32))
    P.add("eps", np.full((128, 1), EPS, np.float32))
    P.add("one", np.ones((128, 1), np.float32))
    off, n = cst_layout()
    assert P.off == off and P.n == n
    return P.pack()


def pack_cst2():
    P = Packer()
    t1 = np.arange(1, TT + 1, dtype=np.float32)
    for g, win in enumerate((2, 4, 8, 16)):
        P.add(f"rc{g}", np.tile((1.0 / np.minimum(t1, float(win)))[None, :], (128, 1)))
    i = np.arange(128)
    same = (i[:, None] // 64) == (i[None, :] // 64)
    Ubd = ((i[:, None] <= i[None, :]) & same).astype(np.float32)
    P.add("Ubd", Ubd)
    P.add("nUbd", -Ubd)
    P.add("SLbd", ((i[:, None] > i[None, :]) & same).astype(np.float32))
    P.add("MnS", np.where((i[:, None] > i[None, :]) & same, 0.0, -60000.0).astype(np.float32))
    P.add("MnIT", np.where((i[None, :] >= i[:, None]) & same, 0.0, -60000.0).astype(np.float32))
    off, n = cst_layout(2)
    assert P.off == off and P.n == n
    return P.pack()


class Prog:
    def __init__(self, T, depth, dbg=()):
        self.T = T
        self.depth = depth
        self.NT = T // TT
        self.dbg = set(dbg)
        nc = bass.Bass("TRN2", target_bir_lowering=False)
        self.nc = nc
        self.kb = KB(nc)
        self.pvoff, npv = pv_layout(depth)
        self.coff, ncst = cst_layout()
        ne = (depth + 1) // 2
        no = depth // 2

        def din(name, shape, dt=F32):
            return nc.dram_tensor(name, list(shape), dt, kind="ExternalInput").ap()

        self.x_in = din("x", [T, D])
        self.pv_in = din("pv", [128, npv])
        self.cst_in = din("cst", [128, ncst])
        self.coff2, ncst2 = cst_layout(2)
        self.cst2_in = din("cst2", [128, ncst2])
        self.ada_w = din("ada_w", [depth, D, 6 * D])
        self.ffn_w_up = din("ffn_w_up", [depth, D, 2 * DFF])
        self.ffn_w_down = din("ffn_w_down", [depth, DFF, D])
        self.y_out = nc.dram_tensor("y", [T, D], F32, kind="ExternalOutput").ap()
        self.xs = self.scratch("xs", [KC, 128, T], F32)
        self.act = self.scratch("act", [FC, 128, T], BF16)
        self.hs = self.scratch("hs", [KC, 128, T], BF16)
        self.qk = self.scratch("qk", [3, 16, 128, T], BF16)
        self.vv = self.scratch("vv", [3, T, D], BF16)
        self.attn = self.scratch("attn", [KC, 128, T], BF16)
        self.qkvT = self.scratch("qkvT", [12, 128, T], F32)
        self.kvtok = self.scratch("kvtok", [T, D], F32)
        self.ztok = self.scratch("ztok", [T, 512], F32)
        self.bgtok = self.scratch("bgtok", [T, 8], F32)
        self.oab = self.scratch("oab", [KC, 128, T], BF16)
        if ne > 0:
            self.ev_w_in = din("ev_w_in", [ne, D, 2568])
            self.ev_w_out = din("ev_w_out", [ne, D, D])
            self.pool_w = din("pool_w", [ne, 4, 128, 128])
        if no > 0:
            self.od_w_in = din("od_w_in", [no, D, 9 * D])
            self.od_w_out = din("od_w_out", [no, D, D])
            self.alibi = din("alibi", [128, 3 * 8 * 4 * 128], BF16)

    def sb(self, st, name, shape, dt):
        self.kb.uid += 1
        return st.enter_context(self.nc.sbuf_tensor(f"{name}_{self.kb.uid}", list(shape), dt))

    def pst(self, st, name, shape, dt):
        self.kb.uid += 1
        return st.enter_context(self.nc.psum_tensor(f"{name}_{self.kb.uid}", list(shape), dt))

    def scratch(self, name, shape, dt):
        kind = "ExternalOutput" if name in self.dbg else "Internal"
        return self.nc.dram_tensor(name, list(shape), dt, kind=kind).ap()

    def setup_globals(self, st):
        nc, kb = self.nc, self.kb
        _, npv = pv_layout(self.depth)
        _, ncst = cst_layout()
        self.pv = self.sb(st, "pv", [128, npv], F32)
        self.cst = self.sb(st, "cst", [128, ncst], F32)
        self.mod = self.sb(st, "mod", [128, self.depth, 64], F32)
        self.b_pv, self.b_cst, self.b_mod = kb.buf("pv"), kb.buf("cst"), kb.buf("mod")
        kb.dma("sp", self.pv[:], self.pv_in[:, :], writes=[self.b_pv])
        kb.dma("sp", self.cst[:], self.cst_in[:, :], writes=[self.b_cst])
        self.ps = []
        self.b_ps = []
        for i in range(8):
            self.ps.append(self.pst(st, f"ps{i}", [128, 512], F32))
            self.b_ps.append(kb.buf("ps"))

    def P(self, name, lo=0, hi=None):
        o, w = self.pvoff[name]
        hi = w if hi is None else hi
        return self.pv[:, o + lo:o + hi]

    def load_c2(self, st, first, last):
        o0 = self.coff2[first][0]
        o1 = self.coff2[last][0] + self.coff2[last][1]
        t = self.sb(st, "c2", [128, o1 - o0], F32)
        b = self.kb.buf("c2")
        self.kb.dma("sp", t[:], self.cst2_in[:, o0:o1], writes=[b])

        def acc(name):
            o, w = self.coff2[name]
            return t[:, o - o0:o - o0 + w]
        return acc, b

    def C(self, name, lo=0, hi=None):
        o, w = self.coff[name]
        hi = w if hi is None else hi
        return self.cst[:, o + lo:o + hi]

    def M(self, i, which, kc):
        base = {"sh_m": 0, "sc_m": 8, "g_m": 16, "sh_f": 24, "sc_f": 32, "g_f": 40,
                "gs_m": 48, "gs_f": 56}[which]
        return self.mod[:, i, base + kc:base + kc + 1]

    def phase_mod(self):
        nc, kb = self.nc, self.kb
        with ExitStack() as st:
            cs = self.sb(st, "cs", [128, 8], F32)
            b_cs = kb.buf("cs")
            wst = self.sb(st, "adaw", [128, 2, 8, 512], F32)
            b_w = kb.bufs(2, "adaw")
            kb.op("act", [self.b_pv], [b_cs],
                  lambda e: e.activation(out=cs[:], in_=self.P("c"), func=AF.Silu))
            n = 0
            for i in range(self.depth):
                for g in range(12):
                    s = n % 2
                    n += 1
                    src = self.ada_w[i, :, g * 512:(g + 1) * 512].rearrange("(k p) c -> p k c", p=128)
                    kb.dma("sp", wst[:, s], src, writes=[b_w[s]])
                    pb = self.b_ps[s]
                    pt = self.ps[s]
                    for cc in range(4):
                        for k in range(8):
                            kb.op("pe", [b_w[s], b_cs], [pb],
                                  lambda e, cc=cc, k=k: e.matmul(
                                      pt[:, cc:cc + 1], lhsT=wst[:, s, k, cc * 128:(cc + 1) * 128],
                                      rhs=cs[:, k:k + 1], start=(k == 0), stop=(k == 7)))
                    o, _ = self.pvoff[f"adab{i}"]
                    kb.op("dve", [pb, self.b_pv], [self.b_mod],
                          lambda e, g=g, o=o, i=i: e.tensor_tensor(
                              out=self.mod[:, i, g * 4:(g + 1) * 4], in0=pt[:, 0:4],
                              in1=self.pv[:, o + g * 4:o + (g + 1) * 4], op=ALU.add))
                for nm, sc, dst in ((f"nm{i}", 8, 48), (f"nf{i}", 32, 56)):
                    kb.op("dve", [self.b_mod, self.b_pv], [self.b_mod],
                          lambda e, nm=nm, sc=sc, dst=dst, i=i: e.scalar_tensor_tensor(
                              out=self.mod[:, i, dst:dst + 8], in0=self.mod[:, i, sc:sc + 8],
                              scalar=1.0, in1=self.P(nm), op0=ALU.add, op1=ALU.mult))
            kb.barrier()

    def phase_in(self, norm_next=None):
        nc, kb = self.nc, self.kb
        with ExitStack() as st:
            xin = self.sb(st, "xin", [128, 2, 4, D], F32)
            xo = self.sb(st, "xo", [128, 2, KC, TT], F32)
            b_in, b_o = kb.bufs(2, "xin"), kb.bufs(2, "xo")
            if norm_next is not None:
                hN = self.sb(st, "hN", [128, 2, KC, TT], BF16)
                bhN = kb.bufs(2, "hN")
                nrN = self.norm_alloc(st)
            for j in range(self.NT):
                s = j % 2
                kb.dma("sp", xin[:, s], self.x_in[j * TT:(j + 1) * TT, :].rearrange("(a p) d -> p a d", p=128),
                       writes=[b_in[s]])
                for kc in range(KC):
                    pi = kc % 4
                    for a in range(4):
                        kb.op("pe", [b_in[s], self.b_cst], [self.b_ps[pi]],
                              lambda e, a=a, kc=kc, pi=pi: e.transpose(
                                  self.ps[pi][:, a * 128:(a + 1) * 128],
                                  xin[:, s, a, kc * 128:(kc + 1) * 128], self.C("ident")))
                    if kc % 2 == 0:
                        kb.op("dve", [self.b_ps[pi]], [b_o[s]],
                              lambda e, kc=kc, pi=pi: e.tensor_copy(out=xo[:, s, kc, :], in_=self.ps[pi][:]))
                    else:
                        kb.op("act", [self.b_ps[pi]], [b_o[s]],
                              lambda e, kc=kc, pi=pi: e.copy(out=xo[:, s, kc, :], in_=self.ps[pi][:]))
                kb.dma("pool", self.xs[:, :, j * TT:(j + 1) * TT].rearrange("k p t -> p k t"), xo[:, s],
                       reads=[b_o[s]])
                if norm_next is not None:
                    self.norm_tile(nrN, xo[:, s], b_o[s], hN[:, s], bhN[s], norm_next[0], norm_next[1], 4 + s)
                    kb.dma("pool", self.hs[:, :, j * TT:(j + 1) * TT].rearrange("k p t -> p k t"), hN[:, s],
                           reads=[bhN[s]])
            kb.barrier()

    def phase_out(self):
        nc, kb = self.nc, self.kb
        with ExitStack() as st:
            X = self.sb(st, "X", [128, 2, KC, TT], F32)
            yo = self.sb(st, "yo", [128, 2, 4, D], F32)
            b_X, b_y = kb.bufs(2, "X"), kb.bufs(2, "yo")
            n = 0
            for j in range(self.NT):
                s = j % 2
                kb.dma("sp", X[:, s], self.xs[:, :, j * TT:(j + 1) * TT].rearrange("k p t -> p k t"),
                       writes=[b_X[s]])
                for a in range(4):
                    for hf in range(2):
                        pi = n % 4
                        n += 1
                        for q in range(4):
                            kc = hf * 4 + q
                            kb.op("pe", [b_X[s], self.b_cst], [self.b_ps[pi]],
                                  lambda e, a=a, kc=kc, q=q, pi=pi: e.transpose(
                                      self.ps[pi][:, q * 128:(q + 1) * 128],
                                      X[:, s, kc, a * 128:(a + 1) * 128], self.C("ident")))
                        if n % 2 == 0:
                            kb.op("dve", [self.b_ps[pi]], [b_y[s]],
                                  lambda e, a=a, hf=hf, pi=pi: e.tensor_copy(
                                      out=yo[:, s, a, hf * 512:(hf + 1) * 512], in_=self.ps[pi][:]))
                        else:
                            kb.op("act", [self.b_ps[pi]], [b_y[s]],
                                  lambda e, a=a, hf=hf, pi=pi: e.copy(
                                      out=yo[:, s, a, hf * 512:(hf + 1) * 512], in_=self.ps[pi][:]))
                kb.dma("pool", self.y_out[j * TT:(j + 1) * TT, :].rearrange("(a p) d -> p a d", p=128), yo[:, s],
                       reads=[b_y[s]])
            kb.barrier()

    def load_w(self, st, name, Wd, nk, C, SC=1408):
        nc, kb = self.nc, self.kb
        W = self.sb(st, name, [128, nk, C], BF16)
        bW = kb.buf(name)
        stg = self.sb(st, name + "_stg", [128, 2, SC], F32)
        b_s = kb.bufs(2, "stg")
        n = 0
        for k in range(nk):
            for c0 in range(0, C, SC):
                c1 = min(C, c0 + SC)
                s = n % 2
                kb.dma("sp", stg[:, s, 0:c1 - c0], Wd[k * 128:(k + 1) * 128, c0:c1], writes=[b_s[s]])
                eng = ("dve", "act")[n % 2]
                if eng == "act":
                    kb.op("act", [b_s[s]], [bW], lambda e, k=k, c0=c0, c1=c1, s=s: e.copy(
                        out=W[:, k, c0:c1], in_=stg[:, s, 0:c1 - c0]))
                else:
                    kb.op(eng, [b_s[s]], [bW], lambda e, k=k, c0=c0, c1=c1, s=s: e.tensor_copy(
                        out=W[:, k, c0:c1], in_=stg[:, s, 0:c1 - c0]))
                n += 1
        return W, bW

    def norm_alloc(self, st):
        nc, kb = self.nc, self.kb
        r = {}
        r["sq"] = self.sb(st, "sq", [128, 2, TT], F32)
        r["b_sq"] = kb.bufs(2, "sq")
        r["rs"] = self.sb(st, "rs", [128, 2, TT], F32)
        r["b_rs"] = kb.bufs(2, "rs")
        r["t"] = self.sb(st, "nt", [128, 2, TT], F32)
        r["b_t"] = kb.bufs(2, "nt")
        r["n"] = 0
        return r

    def norm_tile(self, r, X, bX, h, bh, li, which, psi):
        kb = self.kb
        pb, pt = self.b_ps[psi], self.ps[psi]
        gsn, shn = ("gs_m", "sh_m") if which == "m" else ("gs_f", "sh_f")
        for kc in range(KC):
            s = r["n"] % 2
            r["n"] += 1
            kb.op("act", [bX], [r["b_sq"][s]],
                  lambda e, kc=kc, s=s: e.activation(out=r["sq"][:, s, :], in_=X[:, kc, :], func=AF.Square))
            kb.op("pe", [r["b_sq"][s], self.b_cst], [pb],
                  lambda e, kc=kc, s=s: e.matmul(pt[:], lhsT=self.C("ones"), rhs=r["sq"][:, s, :],
                                                 start=(kc == 0), stop=(kc == KC - 1)))
        kb.op("act", [pb], [r["b_rs"][0]],
              lambda e: e.activation(out=r["rs"][:, 0, :], in_=pt[:], func=AF.Ln, bias=self.C("eps"),
                                     scale=1.0 / D))
        kb.op("act", [r["b_rs"][0]], [r["b_rs"][1]],
              lambda e: e.activation(out=r["rs"][:, 1, :], in_=r["rs"][:, 0, :], func=AF.Exp, scale=-0.5))
        for kc in range(KC):
            s = r["n"] % 2
            r["n"] += 1
            kb.op("dve", [bX, r["b_rs"][1], self.b_mod], [r["b_t"][s]],
                  lambda e, kc=kc, s=s: e.scalar_tensor_tensor(
                      out=r["t"][:, s, :], in0=X[:, kc, :], scalar=self.M(li, gsn, kc),
                      in1=r["rs"][:, 1, :], op0=ALU.mult, op1=ALU.mult))
            kb.op("act", [r["b_t"][s], self.b_mod], [bh],
                  lambda e, kc=kc, s=s: e.activation(
                      out=h[:, kc, :], in_=r["t"][:, s, :], func=AF.Identity, bias=self.M(li, shn, kc), scale=1.0))

    def phase_ffn_up(self, li, xs):
        nc, kb = self.nc, self.kb
        NT = self.NT
        with ExitStack() as st:
            W, bW = self.load_w(st, "wup", self.ffn_w_up[li], KC, 2 * DFF)
            X = self.sb(st, "X", [128, KC, TT], F32)
            bX = kb.buf("X")
            h = self.sb(st, "h", [128, 2, KC, TT], BF16)
            bh = kb.bufs(2, "h")
            A = self.sb(st, "A", [128, 2, FC, TT], BF16)
            bA = kb.bufs(2, "A")
            G = self.sb(st, "G", [128, 2, TT + 2], F32)
            bG = kb.bufs(2, "G")
            acc = self.sb(st, "acc", [128, 2, TT], F32)
            bacc = kb.bufs(2, "acc")
            sg = self.sb(st, "sg", [128, 2, TT], F32)
            bsg = kb.bufs(2, "sg")
            H = self.sb(st, "H", [128, FC, 2], F32)
            bH = kb.buf("H")
            nr = self.norm_alloc(st)
            kb.op("pool", [], [bH], lambda e: e.memset(H[:], 0.0))
            cwo, _ = self.pvoff[f"fcw{li}"]
            cbo, _ = self.pvoff[f"fcb{li}"]

            def load_x(j):
                kb.dma("sp", X[:], xs[:, :, j * TT:(j + 1) * TT].rearrange("k p t -> p k t"), writes=[bX])

            load_x(0)
            self.norm_tile(nr, X, bX, h[:, 0], bh[0], li, "f", 4)
            n = 0
            for j in range(NT):
                s = j % 2
                if j + 1 < NT:
                    load_x(j + 1)
                for c in range(FC):
                    u = n % 2
                    n += 1
                    pg, pv_ = self.ps[u], self.ps[2 + u]
                    bpg, bpv = self.b_ps[u], self.b_ps[2 + u]
                    for k in range(KC):
                        kb.op("pe", [bW, bh[s]], [bpg], lambda e, k=k, c=c: e.matmul(
                            pg[:], lhsT=W[:, k, c * 128:(c + 1) * 128], rhs=h[:, s, k, :],
                            start=(k == 0), stop=(k == KC - 1)))
                    for k in range(KC):
                        kb.op("pe", [bW, bh[s]], [bpv], lambda e, k=k, c=c: e.matmul(
                            pv_[:], lhsT=W[:, k, DFF + c * 128:DFF + (c + 1) * 128], rhs=h[:, s, k, :],
                            start=(k == 0), stop=(k == KC - 1)))
                    kb.op("act", [bH], [bG[u]], lambda e, c=c, u=u: e.copy(
                        out=G[:, u, 0:2], in_=H[:, c, :]))
                    kb.op("act", [bpg], [bG[u]], lambda e, u=u: e.copy(out=G[:, u, 2:TT + 2], in_=pg[:]))
                    kb.op("act", [bpg], [bH], lambda e, c=c: e.copy(
                        out=H[:, c, :], in_=pg[:, TT - 2:TT]))
                    kb.op("act", [bpg, self.b_pv], [bacc[u]], lambda e, c=c, u=u: e.activation(
                        out=acc[:, u, :], in_=pg[:], func=AF.Identity,
                        scale=self.pv[:, cwo + 44 + c:cwo + 44 + c + 1],
                        bias=self.pv[:, cbo + c:cbo + c + 1]))
                    kb.op("dve", [bG[u], bacc[u], self.b_pv], [bacc[u]], lambda e, c=c, u=u: e.scalar_tensor_tensor(
                        out=acc[:, u, :], in0=G[:, u, 1:TT + 1], scalar=self.pv[:, cwo + 22 + c:cwo + 22 + c + 1],
                        in1=acc[:, u, :], op0=ALU.mult, op1=ALU.add))
                    kb.op("dve", [bG[u], bacc[u], self.b_pv], [bacc[u]], lambda e, c=c, u=u: e.scalar_tensor_tensor(
                        out=acc[:, u, :], in0=G[:, u, 0:TT], scalar=self.pv[:, cwo + c:cwo + c + 1],
                        in1=acc[:, u, :], op0=ALU.mult, op1=ALU.add))
                    kb.op("act", [bacc[u]], [bsg[u]], lambda e, u=u: e.activation(
                        out=sg[:, u, :], in_=acc[:, u, :], func=AF.Silu))
                    kb.op("dve", [bsg[u], bpv], [bA[s]], lambda e, c=c, u=u: e.tensor_tensor(
                        out=A[:, s, c, :], in0=sg[:, u, :], in1=pv_[:], op=ALU.mult))
                    if c == FC // 2 and j + 1 < NT:
                        self.norm_tile(nr, X, bX, h[:, 1 - s], bh[1 - s], li, "f", 4)
                kb.dma("pool", self.act[:, :, j * TT:(j + 1) * TT].rearrange("k p t -> p k t"), A[:, s],
                       reads=[bA[s]])
            kb.barrier()

    def phase_proj_res(self, name, src, nk, Wd, li, gname, xs_in, xs_out, norm_next=None):
        nc, kb = self.nc, self.kb
        NT = self.NT
        with ExitStack() as st:
            W, bW = self.load_w(st, "w" + name, Wd, nk, D, SC=1024)
            X = self.sb(st, "X", [128, 2, KC, TT], F32)
            bX = kb.bufs(2, "X")
            A = self.sb(st, "A", [128, 2, nk, TT], BF16)
            bA = kb.bufs(2, "A")
            if norm_next is not None:
                hN = self.sb(st, "hN", [128, 2, KC, TT], BF16)
                bhN = kb.bufs(2, "hN")
                nrN = self.norm_alloc(st)

            def load(j):
                s = j % 2
                kb.dma("sp", A[:, s], src[:, :, j * TT:(j + 1) * TT].rearrange("k p t -> p k t"), writes=[bA[s]])
                kb.dma("sp", X[:, s], xs_in[:, :, j * TT:(j + 1) * TT].rearrange("k p t -> p k t"), writes=[bX[s]])

            load(0)
            n = 0
            for j in range(NT):
                s = j % 2
                if j + 1 < NT:
                    load(j + 1)
                for oc in range(KC):
                    u = n % 2
                    n += 1
                    pt, pb = self.ps[u], self.b_ps[u]
                    for k in range(nk):
                        kb.op("pe", [bW, bA[s]], [pb], lambda e, k=k, oc=oc: e.matmul(
                            pt[:], lhsT=W[:, k, oc * 128:(oc + 1) * 128], rhs=A[:, s, k, :],
                            start=(k == 0), stop=(k == nk - 1)))
                    kb.op("dve", [pb, bX[s], self.b_mod], [bX[s]], lambda e, oc=oc: e.scalar_tensor_tensor(
                        out=X[:, s, oc, :], in0=pt[:], scalar=self.M(li, gname, oc), in1=X[:, s, oc, :],
                        op0=ALU.mult, op1=ALU.add))
                kb.dma("pool", xs_out[:, :, j * TT:(j + 1) * TT].rearrange("k p t -> p k t"), X[:, s],
                       reads=[bX[s]])
                if norm_next is not None:
                    self.norm_tile(nrN, X[:, s], bX[s], hN[:, s], bhN[s], norm_next[0], norm_next[1], 4 + s)
                    kb.dma("pool", self.hs[:, :, j * TT:(j + 1) * TT].rearrange("k p t -> p k t"), hN[:, s],
                           reads=[bhN[s]])
            kb.barrier()

    def phase_norm(self, li, which, xs):
        nc, kb = self.nc, self.kb
        with ExitStack() as st:
            X = self.sb(st, "X", [128, 2, KC, TT], F32)
            bX = kb.bufs(2, "X")
            h = self.sb(st, "h", [128, 2, KC, TT], BF16)
            bh = kb.bufs(2, "h")
            nr = self.norm_alloc(st)
            for j in range(self.NT):
                s = j % 2
                kb.dma("sp", X[:, s], xs[:, :, j * TT:(j + 1) * TT].rearrange("k p t -> p k t"), writes=[bX[s]])
                self.norm_tile(nr, X[:, s], bX[s], h[:, s], bh[s], li, which, 4 + s)
                kb.dma("pool", self.hs[:, :, j * TT:(j + 1) * TT].rearrange("k p t -> p k t"), h[:, s],
                       reads=[bh[s]])
            kb.barrier()

    def phase_odd_proj(self, li, g):
        nc, kb = self.nc, self.kb
        o = li // 2
        with ExitStack() as st:
            W, bW = self.load_w(st, "wod", self.od_w_in[o][:, g * 3072:(g + 1) * 3072], KC, 3072, SC=1024)
            h = self.sb(st, "h", [128, 2, KC, TT], BF16)
            bh = kb.bufs(2, "h")
            QK = self.sb(st, "QK", [128, 2, 16, TT], BF16)
            bQK = kb.bufs(2, "QK")
            Vt = self.sb(st, "Vt", [128, 2, 4, D], BF16)
            bVt = kb.bufs(2, "Vt")
            sq = self.sb(st, "sq", [128, 4, TT], BF16)
            bsq = kb.bufs(4, "sq")
            rs = self.sb(st, "rs", [128, 2, TT], F32)
            brs = kb.bufs(2, "rs")
            ri = self.sb(st, "ri", [128, 2, TT], F32)
            bri = kb.bufs(2, "ri")
            onesb = self.sb(st, "onesb", [128, 128], BF16)
            bonesb = kb.buf("onesb")
            kb.op("dve", [self.b_cst], [bonesb], lambda e: e.tensor_copy(out=onesb[:], in_=self.C("ones")))
            gn = self.sb(st, "gn", [128, 2], F32)
            bgn = kb.buf("gn")
            kb.op("pool", [self.b_pv], [bgn], lambda e: e.tensor_scalar(
                out=gn[:, 0:1], in0=self.P(f"qn{li}"), scalar1=128.0 ** -0.5, scalar2=None, op0=ALU.mult))
            kb.op("pool", [self.b_pv], [bgn], lambda e: e.tensor_copy(out=gn[:, 1:2], in_=self.P(f"kn{li}")))

            def load(j):
                s = j % 2
                kb.dma("sp", h[:, s], self.hs[:, :, j * TT:(j + 1) * TT].rearrange("k p t -> p k t"), writes=[bh[s]])

            load(0)
            n = 0
            nv = 0
            for j in range(self.NT):
                s = j % 2
                if j + 1 < self.NT:
                    load(j + 1)
                def partA(c, u):
                    pp, bpp = self.ps[u], self.b_ps[u]
                    for k in range(KC):
                        kb.op("pe", [bW, bh[s]], [bpp], lambda e, k=k: e.matmul(
                            pp[:], lhsT=W[:, k, c * 128:(c + 1) * 128], rhs=h[:, s, k, :],
                            start=(k == 0), stop=(k == KC - 1)))
                    kb.op("act", [bpp], [bsq[u]], lambda e: e.activation(
                        out=sq[:, u, :], in_=pp[:], func=AF.Square))

                def partB(c, u):
                    pp, bpp = self.ps[u], self.b_ps[u]
                    um = u % 2
                    pm, bpm = self.ps[4 + um], self.b_ps[4 + um]
                    kb.op("pe", [bsq[u], bonesb], [bpm], lambda e: e.matmul(
                        pm[:], lhsT=onesb[:], rhs=sq[:, u, :], start=True, stop=True))
                    kb.op("act", [bpm], [brs[um]], lambda e: e.activation(
                        out=rs[:, um, :], in_=pm[:], func=AF.Ln, bias=self.C("eps"), scale=1.0 / 128))
                    kb.op("act", [brs[um]], [bri[um]], lambda e: e.activation(
                        out=ri[:, um, :], in_=rs[:, um, :], func=AF.Exp, scale=-0.5))
                    kb.op("dve", [bpp, bri[um], bgn], [bQK[s]], lambda e: e.scalar_tensor_tensor(
                        out=QK[:, s, c, :], in0=pp[:], scalar=gn[:, c // 8:c // 8 + 1], in1=ri[:, um, :],
                        op0=ALU.mult, op1=ALU.mult))

                for c in range(16):
                    partA(c, (n + c) % 4)
                    if c > 0:
                        partB(c - 1, (n + c - 1) % 4)
                partB(15, (n + 15) % 4)
                n += 16
                kb.dma("pool", self.qk[g, :, :, j * TT:(j + 1) * TT].rearrange("k p t -> p k t"), QK[:, s],
                       reads=[bQK[s]])
                for a in range(4):
                    for hf in range(2):
                        u = 6 + nv % 2
                        nv += 1
                        pp, bpp = self.ps[u], self.b_ps[u]
                        for k in range(KC):
                            kb.op("pe", [bW, bh[s]], [bpp], lambda e, k=k, a=a, hf=hf: e.matmul(
                                pp[:], lhsT=h[:, s, k, a * 128:(a + 1) * 128],
                                rhs=W[:, k, 2048 + hf * 512:2048 + (hf + 1) * 512],
                                start=(k == 0), stop=(k == KC - 1)))
                        kb.op("act", [bpp], [bVt[s]], lambda e, a=a, hf=hf: e.copy(
                            out=Vt[:, s, a, hf * 512:(hf + 1) * 512], in_=pp[:]))
                kb.dma("pool", self.vv[g, j * TT:(j + 1) * TT, :].rearrange("(a p) c -> p a c", p=128), Vt[:, s],
                       reads=[bVt[s]])
            kb.barrier()

    def phase_attn(self, li):
        nc, kb = self.nc, self.kb
        T = self.T
        with ExitStack() as st:
            q = self.sb(st, "q", [128, 2, T], BF16)
            k_ = self.sb(st, "k", [128, 2, T], BF16)
            V = self.sb(st, "V", [128, 2, T // 128, 128], BF16)
            bq, bk, bV = kb.bufs(2, "q"), kb.bufs(2, "k"), kb.bufs(2, "V")
            ab = self.sb(st, "ab", [128, 2, 3, 4, 128], BF16)
            bab = kb.bufs(2, "ab")
            acc = self.sb(st, "acc", [128, 2, T], F32)
            bacc = kb.buf("acc")
            oo = self.sb(st, "oo", [128, T], BF16)
            boo = kb.buf("oo")
            PT = self.sb(st, "PT", [128, 4, 256], BF16)
            bPT = kb.bufs(4, "PT")
            cb = self.sb(st, "cb", [128, 256], BF16)
            bcb = kb.buf("cb")
            kb.op("dve", [self.b_cst], [bcb], lambda e: e.tensor_copy(out=cb[:, 0:128], in_=self.C("ident")))
            kb.op("dve", [self.b_cst], [bcb], lambda e: e.tensor_copy(out=cb[:, 128:256], in_=self.C("ones")))
            it = 0
            nu = 0
            for hd in range(8):
                hs_ = hd % 2
                kb.dma("sp", ab[:, hs_], self.alibi.rearrange("p (g h f) -> p g h f", g=3, h=8)[:, :, hd, :]
                       .rearrange("p g (i a) -> p g i a", i=4), writes=[bab[hs_]])
                for g, dil in enumerate((1, 4, 16)):
                    s = it % 2
                    it += 1
                    kb.dma("sp", q[:, s, :], self.qk[g, hd, :, :], writes=[bq[s]])
                    kb.dma("sp", k_[:, s, :], self.qk[g, 8 + hd, :, :], writes=[bk[s]])
                    vsrc = self.vv[g, :, hd * 128:(hd + 1) * 128].rearrange("(n p r) c -> p r n c", p=128, r=dil)
                    nb = T // dil // 128
                    for r in range(dil):
                        for n0 in range(0, nb, 16):
                            n1 = min(nb, n0 + 16)
                            kb.dma("sp", V[:, s, r * nb + n0:r * nb + n1, :], vsrc[:, r, n0:n1, :], writes=[bV[s]])
                    units = [(r, n) for r in range(dil) for n in range(nb)]

                    def sl(r, nn):
                        st0 = r + dil * 128 * nn
                        return slice(st0, st0 + dil * 127 + 1, dil)

                    def emit_S(idx, u):
                        r, n = units[idx]
                        pS, bpS = self.ps[u], self.b_ps[u]
                        lo = 0 if n > 0 else 128
                        for bi, nn in ((0, n - 1), (1, n)):
                            if nn < 0:
                                continue
                            cols = slice(bi * 128, (bi + 1) * 128)
                            kb.op("pe", [bk[s], bq[s]], [bpS], lambda e: e.matmul(
                                pS[:, cols], lhsT=k_[:, s, sl(r, nn)], rhs=q[:, s, sl(r, n)], start=True, stop=False))
                            kb.op("pe", [bcb, bab[hs_]], [bpS], lambda e: e.matmul(
                                pS[:, cols], lhsT=cb[:, 0:128], rhs=ab[:, hs_, g, 2 * bi, :], start=False, stop=False))
                            kb.op("pe", [bcb, bab[hs_]], [bpS], lambda e: e.matmul(
                                pS[:, cols], lhsT=cb[:, 0:128], rhs=ab[:, hs_, g, 2 * bi + 1, :], start=False, stop=True))
                        kb.op("act", [bpS], [bPT[u]], lambda e: e.activation(
                            out=PT[:, u, lo:256], in_=pS[:, lo:256], func=AF.Exp))

                    def emit_PV(idx, u):
                        r, n = units[idx]
                        pO, bpO = self.ps[4 + u], self.b_ps[4 + u]
                        blks = [(bi, nn) for bi, nn in ((0, n - 1), (1, n)) if nn >= 0]
                        for half, is_l in ((0, False), (1, True)):
                            for ii, (bi, nn) in enumerate(blks):
                                lhs = cb[:, 128:256] if is_l else V[:, s, r * nb + nn, :]
                                kb.op("pe", [bV[s], bcb, bPT[u]], [bpO], lambda e: e.matmul(
                                    pO[:, half * 128:(half + 1) * 128], lhsT=lhs, rhs=PT[:, u, bi * 128:(bi + 1) * 128],
                                    start=(ii == 0), stop=(ii == len(blks) - 1)))
                        dst = acc[:, :, sl(r, n)]
                        src = pO[:, 0:256].rearrange("p (h a) -> p h a", h=2)
                        if g == 0:
                            kb.op("act", [bpO], [bacc], lambda e: e.copy(out=dst, in_=src))
                        else:
                            kb.op("dve", [bpO, bacc], [bacc], lambda e: e.tensor_tensor(
                                out=dst, in0=src, in1=dst, op=ALU.add))

                    LA = 2
                    base = nu
                    for i in range(min(LA, len(units))):
                        emit_S(i, (base + i) % 4)
                    for i in range(len(units)):
                        if i + LA < len(units):
                            emit_S(i + LA, (base + i + LA) % 4)
                        emit_PV(i, (base + i) % 4)
                    nu += len(units)
                kb.op("act", [bacc], [bacc], lambda e: e.activation(out=acc[:, 1, :], in_=acc[:, 1, :], func=AF.Ln))
                kb.op("act", [bacc], [bacc], lambda e: e.activation(out=acc[:, 1, :], in_=acc[:, 1, :], func=AF.Exp, scale=-1.0))
                kb.op("dve", [bacc], [boo], lambda e: e.tensor_tensor(
                    out=oo[:], in0=acc[:, 0, :], in1=acc[:, 1, :], op=ALU.mult))
                kb.dma("pool", self.attn[hd, :, :], oo[:], reads=[boo])
            kb.barrier()

    def phase_even_proj(self, li):
        nc, kb = self.nc, self.kb
        e_ = li // 2
        NT = self.NT
        with ExitStack() as st:
            W, bW = self.load_w(st, "wev", self.ev_w_in[e_], KC, 2568, SC=1284)
            PW = self.sb(st, "PW", [128, 4, 128], BF16)
            bPW = kb.buf("PW")
            pws = self.sb(st, "pws", [128, 4, 128], F32)
            bpws = kb.buf("pws")
            kb.dma("sp", pws[:], self.pool_w[e_].rearrange("g c d -> c g d"), writes=[bpws])
            kb.op("dve", [bpws], [bPW], lambda e: e.tensor_copy(out=PW[:], in_=pws[:]))
            h = self.sb(st, "h", [128, 2, KC, TT], BF16)
            bh = kb.bufs(2, "h")
            QKV = self.sb(st, "QKV", [128, 1, 12, TT], F32)
            bQKV = kb.bufs(1, "QKV") * 2
            KVt = self.sb(st, "KVt", [128, 1, 4, D], F32)
            bKVt = kb.bufs(1, "KVt") * 2
            Zt = self.sb(st, "Zt", [128, 1, 4, 512], F32)
            bZt = kb.bufs(1, "Zt") * 2
            BG = self.sb(st, "BG", [128, 2, 4, 8], F32)
            bBG = kb.bufs(2, "BG")
            OB = self.sb(st, "OB", [128, 2, 4, TT], BF16)
            bOB = kb.bufs(2, "OB")
            G = self.sb(st, "G", [128, 4, TT + 3], F32)
            bG = kb.bufs(4, "G")
            acc = self.sb(st, "acc", [128, 4, TT], F32)
            bacc = kb.bufs(4, "acc")
            sg = self.sb(st, "sg", [128, 4, TT], F32)
            bsg = kb.bufs(4, "sg")
            sq = self.sb(st, "sq", [128, 4, TT], BF16)
            bsq = kb.bufs(4, "sq")
            rs = self.sb(st, "rs", [128, 4, TT], F32)
            brs = kb.bufs(4, "rs")
            Hc = self.sb(st, "Hc", [128, 12, 3], F32)
            bHc = kb.buf("Hc")
            PB = self.sb(st, "PB", [128, 2, TT + 15], F32)
            bPB = kb.bufs(2, "PB")
            S2 = self.sb(st, "S2", [128, 2, TT + 15], F32)
            bS2 = kb.bufs(2, "S2")
            S4 = self.sb(st, "S4", [128, 2, TT + 15], F32)
            bS4 = kb.bufs(2, "S4")
            Hp = self.sb(st, "Hp", [128, 4, 15], F32)
            bHp = kb.buf("Hp")
            pl = self.sb(st, "pl", [128, 2, TT], BF16)
            bpl = kb.bufs(2, "pl")
            bt = self.sb(st, "bt", [128, 4, 8], F32)
            bbt = kb.bufs(4, "bt")
            ea = self.sb(st, "ea", [128, 4], F32)
            bea = kb.buf("ea")
            onesb = self.sb(st, "onesb", [128, 128], BF16)
            bonesb = kb.buf("onesb")
            kb.op("dve", [self.b_cst], [bonesb], lambda e: e.tensor_copy(out=onesb[:], in_=self.C("ones")))
            kb.op("pool", [], [bHc], lambda e: e.memset(Hc[:], 0.0))
            kb.op("pool", [], [bHp], lambda e: e.memset(Hp[:], 0.0))
            kb.op("pool", [], bS2, lambda e: e.memset(S2[:], 0.0))
            kb.op("pool", [], bS4, lambda e: e.memset(S4[:], 0.0))
            kb.op("act", [self.b_pv], [bea], lambda e: e.activation(out=ea[:], in_=self.P(f"alog{li}"), func=AF.Exp))
            gco, _ = self.pvoff[f"gcw{li}"]
            pso, _ = self.pvoff[f"pscale{li}"]
            C2, bc2 = self.load_c2(st, "rc0", "rc3")

            def load(j):
                s = j % 2
                kb.dma("sp", h[:, s], self.hs[:, :, j * TT:(j + 1) * TT].rearrange("k p t -> p k t"), writes=[bh[s]])

            load(0)
            n = 0
            for j in range(NT):
                s = j % 2
                if j + 1 < NT:
                    load(j + 1)
                def partA(c, u):
                    pp, bpp = self.ps[u], self.b_ps[u]
                    for k in range(KC):
                        kb.op("pe", [bW, bh[s]], [bpp], lambda e, k=k: e.matmul(
                            pp[:], lhsT=W[:, k, c * 128:(c + 1) * 128], rhs=h[:, s, k, :],
                            start=(k == 0), stop=(k == KC - 1)))
                    kb.op("act", [bHc], [bG[u]], lambda e: e.copy(out=G[:, u, 0:3], in_=Hc[:, c, :]))
                    kb.op("act", [bpp], [bG[u]], lambda e: e.copy(out=G[:, u, 3:TT + 3], in_=pp[:]))
                    kb.op("act", [bpp], [bHc], lambda e: e.copy(out=Hc[:, c, :], in_=pp[:, TT - 3:TT]))
                    kb.op("act", [bpp, self.b_pv], [bacc[u]], lambda e: e.activation(
                        out=acc[:, u, :], in_=pp[:], func=AF.Copy, scale=self.pv[:, gco + 36 + c:gco + 36 + c + 1]))
                    for kk in range(3):
                        kb.op("dve", [bG[u], bacc[u], self.b_pv], [bacc[u]], lambda e, kk=kk: e.scalar_tensor_tensor(
                            out=acc[:, u, :], in0=G[:, u, kk:TT + kk], scalar=self.pv[:, gco + kk * 12 + c:gco + kk * 12 + c + 1],
                            in1=acc[:, u, :], op0=ALU.mult, op1=ALU.add))
                def partA2(c, u):
                    if c >= 8:
                        kb.op("act", [bacc[u]], [bQKV[s]], lambda e: e.activation(
                            out=QKV[:, 0, c, :], in_=acc[:, u, :], func=AF.Silu))
                    else:
                        kb.op("act", [bacc[u]], [bsg[u]], lambda e: e.activation(
                            out=sg[:, u, :], in_=acc[:, u, :], func=AF.Silu))
                        kb.op("act", [bsg[u]], [bsq[u]], lambda e: e.activation(
                            out=sq[:, u, :], in_=sg[:, u, :], func=AF.Square))

                def partB(c, u):
                    um = u % 2
                    pm, bpm = self.ps[4 + um], self.b_ps[4 + um]
                    kb.op("pe", [bsq[u], bonesb], [bpm], lambda e: e.matmul(
                        pm[:], lhsT=onesb[:], rhs=sq[:, u, :], start=True, stop=True))
                    kb.op("act", [bpm], [bsq[u]], lambda e: e.activation(
                        out=sq[:, u, :], in_=pm[:], func=AF.Ln, bias=self.C("eps"), scale=1.0))
                    kb.op("act", [bsq[u]], [brs[u]], lambda e: e.activation(
                        out=rs[:, u, :], in_=sq[:, u, :], func=AF.Exp, scale=-0.5))
                    sc_ = (128.0 ** -0.5) if c < 4 else 1.0
                    kb.op("dve", [bsg[u], brs[u]], [bQKV[s]], lambda e: e.scalar_tensor_tensor(
                        out=QKV[:, 0, c, :], in0=sg[:, u, :], scalar=sc_, in1=rs[:, u, :],
                        op0=ALU.mult, op1=ALU.mult))

                seq = [("A", 0), ("A", 1), ("A2", 0), ("A", 2), ("A2", 1), ("A", 3), ("A2", 2), ("A2", 3),
                       ("B", 0), ("B", 1), ("A", 4), ("A2", 4), ("B", 2), ("A", 5), ("A2", 5), ("B", 3),
                       ("A", 6), ("A2", 6), ("A", 7), ("A2", 7), ("A", 8), ("B", 4), ("A", 9), ("A2", 8), ("B", 5),
                       ("A", 10), ("A2", 9), ("B", 6), ("A", 11), ("A2", 10), ("B", 7), ("A2", 11)]
                for kind_, c in seq:
                    u = c % 4
                    if kind_ == "A":
                        partA(c, u)
                    elif kind_ == "A2":
                        partA2(c, u)
                    else:
                        partB(c, u)
                n += 12
                kb.dma("pool", self.qkvT[:, :, j * TT:(j + 1) * TT].rearrange("k p t -> p k t"), QKV[:, 0],
                       reads=[bQKV[s]])
                for a in range(4):
                    for hf in range(2):
                        u = 6 + n % 2
                        n += 1
                        pp, bpp = self.ps[u], self.b_ps[u]
                        for q_ in range(4):
                            c = 4 + hf * 4 + q_
                            kb.op("pe", [bQKV[s], self.b_cst], [bpp], lambda e, a=a, c=c, q_=q_: e.transpose(
                                pp[:, q_ * 128:(q_ + 1) * 128], QKV[:, 0, c, a * 128:(a + 1) * 128], self.C("ident")))
                        kb.op("act", [bpp], [bKVt[s]], lambda e, a=a, hf=hf: e.copy(
                            out=KVt[:, 0, a, hf * 512:(hf + 1) * 512], in_=pp[:]))
                kb.dma("pool", self.kvtok[j * TT:(j + 1) * TT, :].rearrange("(a p) c -> p a c", p=128), KVt[:, 0],
                       reads=[bKVt[s]])
                for a in range(4):
                    u = 6 + n % 2
                    n += 1
                    pp, bpp = self.ps[u], self.b_ps[u]
                    for k in range(KC):
                        kb.op("pe", [bW, bh[s]], [bpp], lambda e, k=k, a=a: e.matmul(
                            pp[:], lhsT=h[:, s, k, a * 128:(a + 1) * 128], rhs=W[:, k, 1536:2048],
                            start=(k == 0), stop=(k == KC - 1)))
                    kb.op("act", [bpp], [bZt[s]], lambda e, a=a: e.activation(out=Zt[:, 0, a, :], in_=pp[:], func=AF.Silu))
                for a in range(4):
                    u = 6 + n % 2
                    n += 1
                    pp, bpp = self.ps[u], self.b_ps[u]
                    for k in range(KC):
                        kb.op("pe", [bW, bh[s]], [bpp], lambda e, k=k, a=a: e.matmul(
                            pp[:, 0:8], lhsT=h[:, s, k, a * 128:(a + 1) * 128], rhs=W[:, k, 2048:2056],
                            start=(k == 0), stop=(k == KC - 1)))
                    kb.op("act", [bpp], [bbt[a]], lambda e, a=a: e.copy(out=bt[:, a, 0:8], in_=pp[:, 0:8]))
                for a in range(4):
                    kb.op("act", [bbt[a]], [bBG[s]], lambda e, a=a: e.activation(
                        out=BG[:, s, a, 0:4], in_=bt[:, a, 0:4], func=AF.Sigmoid))
                for a in range(4):
                    kb.op("dve", [bbt[a], self.b_pv], [bbt[a]], lambda e, a=a: e.tensor_tensor(
                        out=bt[:, a, 0:4], in0=bt[:, a, 4:8], in1=self.P(f"dtb{li}"), op=ALU.add))
                for a in range(4):
                    kb.op("act", [bbt[a]], [bbt[a]], lambda e, a=a: e.activation(
                        out=bt[:, a, 4:8], in_=bt[:, a, 0:4], func=AF.Exp))
                for a in range(4):
                    kb.op("act", [bbt[a]], [bbt[a]], lambda e, a=a: e.activation(
                        out=bt[:, a, 0:4], in_=bt[:, a, 4:8], func=AF.Ln, bias=self.C("one"), scale=1.0))
                for a in range(4):
                    kb.op("dve", [bbt[a], bea], [bBG[s]], lambda e, a=a: e.scalar_tensor_tensor(
                        out=BG[:, s, a, 4:8], in0=bt[:, a, 0:4], scalar=-1.0, in1=ea[:], op0=ALU.mult, op1=ALU.mult))
                kb.dma("pool", self.ztok[j * TT:(j + 1) * TT, :].rearrange("(a p) c -> p a c", p=128), Zt[:, 0],
                       reads=[bZt[s]])
                kb.dma("pool", self.bgtok[j * TT:(j + 1) * TT, :].rearrange("(a p) c -> p a c", p=128), BG[:, s],
                       reads=[bBG[s]])
                def pool1(g, win):
                    u = g % 2
                    pp, bpp = self.ps[u], self.b_ps[u]
                    pm, bpm = self.ps[4 + u], self.b_ps[4 + u]
                    for k in range(KC):
                        kb.op("pe", [bW, bh[s]], [bpp], lambda e, k=k, g=g: e.matmul(
                            pp[:], lhsT=W[:, k, 2056 + g * 128:2056 + (g + 1) * 128], rhs=h[:, s, k, :],
                            start=(k == 0), stop=(k == KC - 1)))
                    kb.op("act", [bHp], [bPB[u]], lambda e, g=g, u=u: e.copy(out=PB[:, u, 0:15], in_=Hp[:, g, :]))
                    kb.op("act", [bpp], [bPB[u]], lambda e, u=u: e.copy(out=PB[:, u, 15:TT + 15], in_=pp[:]))
                    kb.op("act", [bpp], [bHp], lambda e, g=g: e.copy(out=Hp[:, g, :], in_=pp[:, TT - 15:TT]))
                    src, bsrc = PB, bPB[u]
                    sh = 1
                    bufs = [(S2, bS2[u]), (S4, bS4[u])]
                    lvl = 0
                    while sh < win:
                        dst, bdst = bufs[lvl % 2]
                        eng = "dve"
                        kb.op(eng, [bsrc], [bdst], lambda e, src=src, dst=dst, sh=sh, u=u: e.tensor_tensor(
                            out=dst[:, u, sh:TT + 15], in0=src[:, u, sh:TT + 15], in1=src[:, u, 0:TT + 15 - sh], op=ALU.add))
                        src, bsrc = dst, bdst
                        sh *= 2
                        lvl += 1
                    if j == 0:
                        kb.op("dve", [bsrc, bc2], [bacc[u]], lambda e, src=src, u=u, g=g: e.tensor_tensor(
                            out=acc[:, u, :], in0=src[:, u, 15:TT + 15], in1=C2(f"rc{g}"), op=ALU.mult))
                        kb.op("dve", [bacc[u], bPB[u]], [bpl[u]], lambda e, u=u: e.tensor_tensor(
                            out=pl[:, u, :], in0=acc[:, u, :], in1=PB[:, u, 15:TT + 15], op=ALU.subtract))
                    else:
                        kb.op("dve", [bsrc, bPB[u]], [bpl[u]], lambda e, src=src, u=u, win=win: e.scalar_tensor_tensor(
                            out=pl[:, u, :], in0=src[:, u, 15:TT + 15], scalar=1.0 / win, in1=PB[:, u, 15:TT + 15],
                            op0=ALU.mult, op1=ALU.subtract))
                def pool2(g):
                    u = g % 2
                    pm, bpm = self.ps[4 + u], self.b_ps[4 + u]
                    kb.op("pe", [bPW, bpl[u]], [bpm], lambda e, g=g, u=u: e.matmul(
                        pm[:], lhsT=PW[:, g, :], rhs=pl[:, u, :], start=True, stop=True))
                    kb.op("act", [bpm, self.b_pv], [bOB[s]], lambda e, g=g: e.activation(
                        out=OB[:, s, g, :], in_=pm[:], func=AF.Copy, scale=self.pv[:, pso + g:pso + g + 1]))
                wins = (2, 4, 8, 16)
                pool1(0, wins[0])
                pool1(1, wins[1])
                pool2(0)
                pool1(2, wins[2])
                pool2(1)
                pool1(3, wins[3])
                pool2(2)
                pool2(3)
                kb.dma("pool", self.oab[4:8, :, j * TT:(j + 1) * TT].rearrange("k p t -> p k t"), OB[:, s],
                       reads=[bOB[s]])
            kb.barrier()

    def phase_gdn(self, li):
        nc, kb = self.nc, self.kb
        T = self.T
        NG = 2
        NSC = T // 128
        ngroups = NSC // NG
        with ExitStack() as st:
            KQ = self.sb(st, "KQ", [128, 2, 8, NG * 128], F32)
            bKQ = kb.bufs(2, "KQ")
            KVT = self.sb(st, "KVT", [128, 2, NG, D], F32)
            bKVT = kb.bufs(2, "KVT")
            BGt = self.sb(st, "BGt", [128, 2, NG, 8], F32)
            bBGt = kb.bufs(2, "BGt")
            ZT = self.sb(st, "ZT", [128, 2, NG, 512], F32)
            bZT = kb.bufs(2, "ZT")
            OA = self.sb(st, "OA", [128, 2, 4, NG * 128], BF16)
            bOA = kb.bufs(2, "OA")
            NCH = 4 * NG
            NTL = 26
            WK = self.sb(st, "WK", [128, NCH, NTL, 128], F32)
            Sst = self.sb(st, "Sst", [128, 4, 128], F32)
            bS = kb.bufs(4, "S")
            vn = self.sb(st, "vn", [128, 4, 128], F32)
            bvn = kb.bufs(4, "vn")
            sm = self.sb(st, "sm", [128, NCH, 8], F32)
            kb.op("pool", [], bS, lambda e: e.memset(Sst[:], 0.0))
            kb.op("pool", [], bvn, lambda e: e.memset(vn[:], 0.0))
            kb.op("pool", [], [], lambda e: e.memset(WK[:], 0.0))
            kb.barrier()
            bq = [[kb.buf("q")] * 4 for _ in range(8)]
            gno, _ = self.pvoff[f"gnrow{li}"]
            names = ["Gt", "dec", "decT", "egr", "Y0", "Y1", "X0", "X1", "R0", "R1", "intraT", "ub", "kbg",
                     "kd0", "kd1", "qg", "u", "wT", "o", "t1", "t2", "junk"]
            TI = {nm: i for i, nm in enumerate(names)}

            class Ch:
                pass

            allq = set(id(x[0]) for x in bq)
            kb_op = kb.op

            def xop(eng, reads, writes, fn):
                if eng != "pe":
                    extra = [r for r in reads if id(r) in allq and r not in writes]
                    writes = list(writes) + extra
                kb_op(eng, reads, writes, fn)

            chain_b = [{nm: kb.buf(nm) for nm in names} for _ in range(NCH)]
            chain_bsm = kb.bufs(NCH, "sm")

            def load(gi):
                s = gi % 2
                t0 = gi * NG * 128
                t1 = t0 + NG * 128
                kb.dma("sp", KQ[:, s], self.qkvT[0:8, :, t0:t1].rearrange("k p t -> p k t"), writes=[bKQ[s]])
                kb.dma("sp", KVT[:, s], self.kvtok[t0:t1, :].rearrange("(m p) c -> p m c", p=128), writes=[bKVT[s]])
                kb.dma("sp", BGt[:, s], self.bgtok[t0:t1, :].rearrange("(m p) c -> p m c", p=128), writes=[bBGt[s]])
                kb.dma("sp", ZT[:, s], self.ztok[t0:t1, :].rearrange("(m p) c -> p m c", p=128), writes=[bZT[s]])

            ident, ones = self.C("ident"), self.C("ones")
            C2, bc2 = self.load_c2(st, "Ubd", "MnIT")
            Ubd, nUbd, SLbd, MnS, MnIT = (C2(x) for x in ("Ubd", "nUbd", "SLbd", "MnS", "MnIT"))
            bc = self.b_cst
            kb.barrier()
            load(0)
            for gi in range(ngroups):
                s = gi % 2
                if gi + 1 < ngroups:
                    load(gi + 1)
                chains = []
                for m in range(NG):
                    for hh in range(4):
                        c = Ch()
                        c.i = m * 4 + hh
                        c.m, c.hh = m, hh
                        c.bank = c.i
                        c.Q = [self.ps[c.bank][:, qq * 128:(qq + 1) * 128] for qq in range(4)]
                        c.bQ = bq[c.bank]
                        c.t = {nm: WK[:, c.i, TI[nm], :] for nm in names}
                        c.b = chain_b[c.i]
                        c.sm = sm[:, c.i, :]
                        c.bsm = chain_bsm[c.i]
                        c.kT = KQ[:, s, 4 + hh, m * 128:(m + 1) * 128]
                        c.qT = KQ[:, s, hh, m * 128:(m + 1) * 128]
                        c.ktok = KVT[:, s, m, hh * 128:(hh + 1) * 128]
                        c.vtok = KVT[:, s, m, 512 + hh * 128:512 + (hh + 1) * 128]
                        c.beta = BGt[:, s, m, hh:hh + 1]
                        c.g = BGt[:, s, m, 4 + hh:5 + hh]
                        c.z = ZT[:, s, m, hh * 128:(hh + 1) * 128]
                        chains.append(c)
                for c in chains:
                    xop("dve", [bBGt[s], bc], [c.b["Gt"]], lambda e, c=c: e.tensor_scalar(
                        out=c.t["Gt"], in0=ones, scalar1=c.g, scalar2=None, op0=ALU.mult))
                    xop("pool", [bBGt[s]], [c.bsm], lambda e, c=c: e.tensor_scalar(
                        out=c.sm[:, 2:3], in0=c.beta, scalar1=-1.0, scalar2=None, op0=ALU.mult))
                for c in chains:
                    Gt = c.t["Gt"]
                    rd = [c.b["Gt"], bc]
                    xop("pe", rd, [c.bQ[2]], lambda e: e.matmul(c.Q[2], lhsT=Gt, rhs=Ubd, start=True, stop=True))
                    xop("pe", [bBGt[s], bc], [c.bQ[3]], lambda e: e.matmul(c.Q[3][:, 0:1], lhsT=Ubd, rhs=c.g, start=True, stop=True))
                    xop("pe", [bBGt[s], bc], [c.bQ[3]], lambda e: e.matmul(c.Q[3][:, 1:2], lhsT=SLbd, rhs=c.g, start=True, stop=True))
                for c in chains:
                    xop("act", [c.bQ[3]], [c.bsm], lambda e: e.copy(out=c.sm[:, 6:7], in_=c.Q[3][:, 0:1]))
                    xop("act", [c.bQ[3]], [c.bsm], lambda e: e.activation(out=c.sm[:, 7:8], in_=c.Q[3][:, 0:1], func=AF.Copy, scale=-1.0))
                    xop("act", [c.bQ[3]], [c.bsm], lambda e: e.activation(out=c.sm[:, 0:2], in_=c.Q[3][:, 0:2], func=AF.Exp))
                    xop("act", [c.bQ[2]], [c.b["egr"]], lambda e: e.activation(out=c.t["egr"], in_=c.Q[2], func=AF.Exp))
                    xop("dve", [c.bQ[2], bc], [c.b["dec"]], lambda e: e.tensor_tensor(
                        out=c.t["dec"], in0=MnS, in1=c.Q[2], op=ALU.subtract))
                    xop("dve", [c.bQ[2], bc], [c.b["decT"]], lambda e: e.tensor_tensor(
                        out=c.t["decT"], in0=c.Q[2], in1=MnIT, op=ALU.add))
                for c in chains:
                    xop("act", [c.b["dec"], c.bsm], [c.b["dec"]], lambda e: e.activation(
                        out=c.t["dec"], in_=c.t["dec"], func=AF.Exp, bias=c.sm[:, 6:7], scale=1.0))
                    xop("act", [c.b["decT"], c.bsm], [c.b["decT"]], lambda e: e.activation(
                        out=c.t["decT"], in_=c.t["decT"], func=AF.Exp, bias=c.sm[:, 7:8], scale=1.0))
                for c in chains:
                    xop("pe", [bKQ[s]], [c.bQ[0]], lambda e: e.matmul(c.Q[0], lhsT=c.kT, rhs=c.kT, start=True, stop=True))
                    xop("pe", [bKQ[s]], [c.bQ[1]], lambda e: e.matmul(c.Q[1], lhsT=c.kT, rhs=c.qT, start=True, stop=True))
                for c in chains:
                    xop("dve", [c.bQ[0], c.bsm, c.b["dec"]], [c.b["Y0"]], lambda e: e.scalar_tensor_tensor(
                        out=c.t["Y0"], in0=c.Q[0], scalar=c.sm[:, 2:3], in1=c.t["dec"], op0=ALU.mult, op1=ALU.mult))
                    xop("dve", [c.bQ[1], c.b["decT"]], [c.b["intraT"]], lambda e: e.tensor_tensor(
                        out=c.t["intraT"], in0=c.Q[1], in1=c.t["decT"], op=ALU.mult))
                    xop("act", [bKVT[s], bBGt[s]], [c.b["ub"]], lambda e: e.activation(
                        out=c.t["ub"], in_=c.vtok, func=AF.Copy, scale=c.beta))
                    xop("pool", [bKVT[s], bBGt[s], c.bsm], [c.b["kbg"]], lambda e: e.tensor_scalar(
                        out=c.t["kbg"], in0=c.ktok, scalar1=c.beta, scalar2=c.sm[:, 0:1], op0=ALU.mult, op1=ALU.mult))
                    xop("act", [bKVT[s], c.bsm], [c.b["kd0"]], lambda e: e.activation(
                        out=c.t["kd0"][0:64, :], in_=c.ktok[0:64, :], func=AF.Copy, scale=c.sm[0:64, 1:2]))
                    xop("act", [bKVT[s], c.bsm], [c.b["kd1"]], lambda e: e.activation(
                        out=c.t["kd1"][64:128, :], in_=c.ktok[64:128, :], func=AF.Copy, scale=c.sm[64:128, 1:2]))
                    xop("pool", [bKQ[s], c.b["egr"]], [c.b["qg"]], lambda e: e.tensor_tensor(
                        out=c.t["qg"], in0=c.qT, in1=c.t["egr"], op=ALU.mult))
                for c in chains:
                    xop("pe", [c.b["Y0"], bc], [c.bQ[2]], lambda e: e.transpose(c.Q[2], c.t["Y0"], ident))
                for c in chains:
                    xop("act", [c.bQ[2]], [c.b["X0"]], lambda e: e.copy(out=c.t["X0"], in_=c.Q[2]))
                    xop("dve", [c.bQ[2], bc], [c.b["R0"]], lambda e: e.tensor_tensor(
                        out=c.t["R0"], in0=c.Q[2], in1=ident, op=ALU.add))
                cur = 0
                for lvl in range(1, 6):
                    nx = 1 - cur
                    X, Y, R = f"X{cur}", f"Y{cur}", f"R{cur}"
                    Xn, Yn, Rn = f"X{nx}", f"Y{nx}", f"R{nx}"
                    for c in chains:
                        if lvl < 5:
                            xop("pe", [c.b[X], c.b[Y]], [c.bQ[0]], lambda e: e.matmul(
                                c.Q[0], lhsT=c.t[Y], rhs=c.t[X], start=True, stop=True))
                        xop("pe", [c.b[X], c.b[Y]], [c.bQ[1]], lambda e: e.matmul(
                            c.Q[1], lhsT=c.t[X], rhs=c.t[Y], start=True, stop=True))
                    for c in chains:
                        if lvl < 5:
                            xop("act", [c.bQ[0]], [c.b[Xn]], lambda e: e.copy(out=c.t[Xn], in_=c.Q[0]))
                        xop("dve", [c.bQ[1]], [c.b[Yn]], lambda e: e.tensor_copy(out=c.t[Yn], in_=c.Q[1]))
                    for c in chains:
                        xop("pe", [c.b[R], c.b[Yn]], [c.bQ[2]], lambda e: e.matmul(
                            c.Q[2], lhsT=c.t[Yn], rhs=c.t[R], start=True, stop=True))
                    for ci, c in enumerate(chains):
                        xop("dve", [c.bQ[2], c.b[R]], [c.b[Rn]], lambda e: e.tensor_tensor(
                            out=c.t[Rn], in0=c.Q[2], in1=c.t[R], op=ALU.add))
                    cur = nx
                Rf = f"R{cur}"
                for c in chains:
                    xop("pe", [c.b[Rf], c.b["ub"]], [c.bQ[0]], lambda e: e.matmul(
                        c.Q[0], lhsT=c.t[Rf], rhs=c.t["ub"], start=True, stop=True))
                    xop("pe", [c.b[Rf], c.b["kbg"]], [c.bQ[1]], lambda e: e.matmul(
                        c.Q[1], lhsT=c.t["kbg"], rhs=c.t[Rf], start=True, stop=True))
                for c in chains:
                    xop("act", [c.bQ[0]], [c.b["u"]], lambda e: e.copy(out=c.t["u"], in_=c.Q[0]))
                    xop("dve", [c.bQ[1]], [c.b["wT"]], lambda e: e.tensor_copy(out=c.t["wT"], in_=c.Q[1]))
                for m in range(NG):
                    cs_ = [c for c in chains if c.m == m]
                    for hf in range(2):
                        rng = slice(64 * hf, 64 * hf + 64)
                        for c in cs_:
                            hh = c.hh
                            xop("pe", [c.b["wT"], bS[hh]], [c.bQ[3]], lambda e: e.matmul(
                                c.Q[3], lhsT=c.t["wT"], rhs=Sst[:, hh, :], start=True, stop=True))
                        for c in cs_:
                            hh = c.hh
                            xop("dve", [c.b["u"], c.bQ[3]], [bvn[hh]], lambda e: e.tensor_tensor(
                                out=vn[rng, hh, :], in0=c.t["u"][rng, :], in1=c.Q[3][rng, :], op=ALU.subtract))
                        for c in cs_:
                            hh = c.hh
                            xop("pe", [c.b["qg"], bS[hh]], [c.bQ[0]], lambda e: e.matmul(
                                c.Q[0], lhsT=c.t["qg"], rhs=Sst[:, hh, :], start=True, stop=False))
                            xop("pe", [c.b["intraT"], bvn[hh]], [c.bQ[0]], lambda e: e.matmul(
                                c.Q[0], lhsT=c.t["intraT"], rhs=vn[:, hh, :], start=False, stop=True))
                            kd = "kd0" if hf == 0 else "kd1"
                            xop("pe", [c.b[kd], bvn[hh]], [c.bQ[1]], lambda e: e.matmul(
                                c.Q[1], lhsT=c.t[kd], rhs=vn[:, hh, :], start=True, stop=True))
                        for c in cs_:
                            hh = c.hh
                            gl = c.t["egr"][:, 64 * hf + 63:64 * hf + 64]
                            xop("dve", [bS[hh], c.b["egr"], c.bQ[1]], [bS[hh]], lambda e: e.scalar_tensor_tensor(
                                out=Sst[:, hh, :], in0=Sst[:, hh, :], scalar=gl, in1=c.Q[1], op0=ALU.mult, op1=ALU.add))
                            xop("act", [c.bQ[0]], [c.b["o"]], lambda e: e.copy(out=c.t["o"][rng, :], in_=c.Q[0][rng, :]))
                for c in chains:
                    xop("act", [c.b["o"]], [c.b["junk"]], lambda e: e.activation(
                        out=c.t["junk"], in_=c.t["o"], func=AF.Square))
                    xop("dve", [c.b["junk"]], [c.bsm], lambda e: e.reduce_sum(
                        out=c.sm[:, 3:4], in_=c.t["junk"], axis=mybir.AxisListType.X))
                    xop("act", [c.bsm], [c.bsm], lambda e: e.activation(
                        out=c.sm[:, 4:5], in_=c.sm[:, 3:4], func=AF.Sqrt, bias=self.C("eps"), scale=1.0 / 128))
                    xop("dve", [c.bsm], [c.bsm], lambda e: e.reciprocal(out=c.sm[:, 5:6], in_=c.sm[:, 4:5]))
                    xop("dve", [c.b["o"], c.bsm, self.b_pv], [c.b["t1"]], lambda e: e.scalar_tensor_tensor(
                        out=c.t["t1"], in0=c.t["o"], scalar=c.sm[:, 5:6], in1=self.pv[:, gno:gno + 128],
                        op0=ALU.mult, op1=ALU.mult))
                    xop("pool", [c.b["t1"], bZT[s]], [c.b["t2"]], lambda e: e.tensor_tensor(
                        out=c.t["t2"], in0=c.t["t1"], in1=c.z, op=ALU.mult))
                for c in chains:
                    xop("pe", [c.b["t2"], bc], [c.bQ[2]], lambda e: e.transpose(c.Q[2], c.t["t2"], ident))
                for c in chains:
                    xop("act", [c.bQ[2]], [bOA[s]], lambda e: e.copy(
                        out=OA[:, s, c.hh, c.m * 128:(c.m + 1) * 128], in_=c.Q[2]))
                t0 = gi * NG * 128
                kb.dma("pool", self.oab[0:4, :, t0:t0 + NG * 128].rearrange("k p t -> p k t"), OA[:, s], reads=[bOA[s]])
            kb.barrier()

    def build(self, plan):
        kb = self.kb
        mixers = ("even", "odd", "evenproj")
        with ExitStack() as st:
            self.setup_globals(st)
            self.phase_mod()
            first_norm = (plan[0][1], "m") if plan and plan[0][0] in mixers else None
            self.phase_in(norm_next=first_norm)
            hs_ready = first_norm is not None
            for idx, step in enumerate(plan):
                kind, li = step
                nxt = plan[idx + 1] if idx + 1 < len(plan) else None
                if kind in mixers:
                    if not hs_ready:
                        self.phase_norm(li, "m", self.xs)
                    hs_ready = False
                if kind in ("evenproj", "even"):
                    self.phase_even_proj(li)
                if kind == "even":
                    self.phase_gdn(li)
                    self.phase_proj_res("eo", self.oab, KC, self.ev_w_out[li // 2], li, "g_m", self.xs, self.xs)
                if kind == "odd":
                    for g in range(3):
                        self.phase_odd_proj(li, g)
                    self.phase_attn(li)
                    self.phase_proj_res("ao", self.attn, KC, self.od_w_out[li // 2], li, "g_m", self.xs, self.xs)
                if kind == "ffn":
                    self.phase_ffn_up(li, self.xs)
                    nn = (nxt[1], "m") if nxt is not None and nxt[0] in mixers else None
                    self.phase_proj_res("dn", self.act, FC, self.ffn_w_down[li], li, "g_f", self.xs, self.xs,
                                        norm_next=nn)
                    hs_ready = nn is not None
            self.phase_out()
        kb.close()
        return self.nc


def pack_alibi():
    j = np.arange(128, dtype=np.float64)[:, None]
    a = np.arange(128, dtype=np.float64)[None, :]
    out = np.zeros((128, 3, 8, 4, 128), dtype=ml_dtypes.bfloat16)
    NEG = -30000.0
    for g, dil in enumerate((1, 4, 16)):
        for h in range(8):
            s = (2.0 ** (-(h + 1))) * dil
            prev = np.where(j >= a, -s * (128 + a - j), NEG)
            cur = np.where(j <= a, -s * (a - j), NEG)
            for idx, m in ((0, prev), (2, cur)):
                hi = m.astype(np.float32).astype(ml_dtypes.bfloat16)
                lo = (m - hi.astype(np.float64)).astype(np.float32).astype(ml_dtypes.bfloat16)
                out[:, g, h, idx, :] = hi
                out[:, g, h, idx + 1, :] = lo
    return np.ascontiguousarray(out.reshape(128, -1))


_T_FULL = 8192
_DEPTH = 4
_NCORES = 8
_ACTIVE = [0, 1, 4, 5]


def full_plan(depth):
    plan = []
    for i in range(depth):
        plan.append(("even" if i % 2 == 0 else "odd", i))
        plan.append(("ffn", i))
    return plan


def make_in_maps(inp, depth, ncores):
    B = inp["x"].shape[0]
    cst = pack_cst()
    alibi = pack_alibi()
    f32 = lambda a: np.ascontiguousarray(np.asarray(a, np.float32))
    shared = {
        "cst": cst, "cst2": pack_cst2(), "alibi": alibi,
        "ada_w": f32(inp["ada_w"]), "ffn_w_up": f32(inp["ffn_w_up"]), "ffn_w_down": f32(inp["ffn_w_down"]),
        "ev_w_in": f32(inp["ev_w_in"]), "ev_w_out": f32(inp["ev_w_out"]), "pool_w": f32(inp["pool_w"]),
        "od_w_in": f32(inp["od_w_in"]), "od_w_out": f32(inp["od_w_out"]),
    }
    maps = []
    zx = None
    for c in range(ncores):
        m = dict(shared)
        if c in _ACTIVE:
            b = _ACTIVE.index(c)
            m["x"] = f32(inp["x"][b])
            m["pv"] = pack_pv(inp, b, depth)
        else:
            if zx is None:
                zx = np.zeros_like(f32(inp["x"][0]))
            m["x"] = zx
            m["pv"] = pack_pv(inp, 0, depth)
        maps.append(m)
    return maps


def kernel(**inputs):
    inp = {k: np.asarray(v) for k, v in inputs.items()}
    B, T, _ = inp["x"].shape
    prog = Prog(T, _DEPTH)
    nc = prog.build(full_plan(_DEPTH))
    maps = make_in_maps(inp, _DEPTH, _NCORES)
    res = run_bass_kernel_spmd(nc, maps, core_ids=list(range(_NCORES)))
    out = np.stack([np.asarray(res.results[_ACTIVE[b]]["y"], np.float32) for b in range(B)], axis=0)
    return out
```
